# Optimizing a Trainium2 kernel written in Bass

```python
import math
import jax, jax.numpy as jnp
from jax import lax
import numpy as np


D_MODEL = 1024
BATCH = 4
SEQ = 4096
DEPTH = 2

HEAD_DIM = 64
N_DIFF_HEADS = (D_MODEL // 2) // (2 * HEAD_DIM)
N_DIL_HEADS = (D_MODEL // 2) // HEAD_DIM
DIFF_WIDTH = N_DIFF_HEADS * 2 * HEAD_DIM
DIL_WIDTH = N_DIL_HEADS * HEAD_DIM
MIX_WIDTH = DIFF_WIDTH + DIL_WIDTH
D_FF = 256 * ((8 * D_MODEL // 3 + 255) // 256)
DILATED_CONFIGS = ((128, 1), (512, 4), (2048, 16))
NUM_BUCKETS = 32
MAX_DISTANCE = 2048
N_BIAS_COLS = 2 * N_DIFF_HEADS + N_DIL_HEADS
Q_BLOCK = 128
EPS = 1e-6

kernel_name = 'hybrid_diffattn_dilated_macaron'


def rmsnorm(x, g):
    x32 = x.astype(jnp.float32)
    y = x32 * lax.rsqrt(jnp.mean(x32 * x32, axis=-1, keepdims=True) + EPS)
    return (y * g.astype(jnp.float32)).astype(x.dtype)


def swiglu(x, w_gate, w_up, w_down):
    return (jax.nn.silu(x @ w_gate) * (x @ w_up)) @ w_down


def t5_bucket(dist):
    n = jnp.maximum(dist, 0)
    max_exact = NUM_BUCKETS // 2
    nf = jnp.maximum(n, 1).astype(jnp.float32)
    large = max_exact + (jnp.log(nf / max_exact) / math.log(MAX_DISTANCE / max_exact)
                         * (NUM_BUCKETS - max_exact)).astype(jnp.int32)
    large = jnp.minimum(large, NUM_BUCKETS - 1)
    return jnp.where(n < max_exact, n, large)


def lambda_init(layer):
    return 0.8 - 0.6 * math.exp(-0.3 * layer)


def diff_attention(q, k, v, lam, bias_table):
    B, H, _, S, Dh = q.shape
    nb = S // Q_BLOCK
    scale = Dh ** -0.5
    k_pos = jnp.arange(S)
    q_blocks = q.reshape(B, H, 2, nb, Q_BLOCK, Dh).transpose(3, 0, 1, 2, 4, 5)
    v32 = v.astype(jnp.float32)
    table = bias_table.astype(jnp.float32)

    def one_block(args):
        q_blk, start = args
        s = jnp.einsum('bhmqd,bhmkd->bhmqk', q_blk, k).astype(jnp.float32) * scale
        dist = (start + jnp.arange(Q_BLOCK))[:, None] - k_pos[None, :]
        bias = table[t5_bucket(dist)].transpose(2, 0, 1).reshape(H, 2, Q_BLOCK, S)
        s = jnp.where(dist >= 0, s + bias, -jnp.inf)
        p = jax.nn.softmax(s, axis=-1)
        a = p[:, :, 0] - lam * p[:, :, 1]
        return jnp.einsum('bhqk,bhkd->bhqd', a, v32)

    out = lax.map(one_block, (q_blocks, jnp.arange(nb) * Q_BLOCK))
    return out.transpose(1, 2, 0, 3, 4).reshape(B, H, S, 2 * Dh)


def dilated_branch(q, k, v, window, dil, bias_table):
    B, H, S, Dh = q.shape
    W = window // dil
    Ld = S // dil
    nb = -(-Ld // W)
    Lp = nb * W
    scale = Dh ** -0.5

    def regroup(t):
        t = t.reshape(B, H, Ld, dil, Dh).transpose(0, 1, 3, 2, 4)
        t = jnp.pad(t, ((0, 0), (0, 0), (0, 0), (0, Lp - Ld), (0, 0)))
        return t.reshape(B, H, dil, nb, W, Dh)

    def with_prev(t):
        prev = jnp.pad(t, ((0, 0), (0, 0), (0, 0), (1, 0), (0, 0), (0, 0)))[:, :, :, :nb]
        return jnp.concatenate([prev, t], axis=4)

    qg = regroup(q)
    kw = with_prev(regroup(k))
    vw = with_prev(regroup(v))
    s = jnp.einsum('bhrnqd,bhrnkd->bhrnqk', qg, kw).astype(jnp.float32) * scale
    a_idx = jnp.arange(W)[:, None]
    c_idx = jnp.arange(2 * W)[None, :]
    dist = a_idx + W - c_idx
    key_idx = jnp.arange(nb)[:, None, None] * W - W + c_idx[None]
    valid = ((dist >= 0) & (dist <= W))[None] & (key_idx >= 0)
    bias = bias_table.astype(jnp.float32)[t5_bucket(dist * dil)].transpose(2, 0, 1)
    s = jnp.where(valid, s + bias[None, :, None, None], -jnp.inf)
    m = jnp.max(s, axis=-1)
    e = jnp.exp(s - m[..., None])
    l = jnp.sum(e, axis=-1)
    o = jnp.einsum('bhrnqk,bhrnkd->bhrnqd', e, vw.astype(jnp.float32)) / l[..., None]

    def restore(t):
        tail = t.shape[5:]
        t = t.reshape((B, H, dil, Lp) + tail)[:, :, :, :Ld]
        t = jnp.moveaxis(t, 2, 3)
        return t.reshape((B, H, S) + tail)

    return restore(o), restore(m), restore(l)


def dilated_attention(q, k, v, bias_table):
    outs = [dilated_branch(q, k, v, w, d, bias_table) for (w, d) in DILATED_CONFIGS]
    m_all = jnp.stack([br[1] for br in outs])
    l_all = jnp.stack([br[2] for br in outs])
    o_all = jnp.stack([br[0] for br in outs])
    wts = l_all * jnp.exp(m_all - jnp.max(m_all, axis=0, keepdims=True))
    wts = wts / jnp.sum(wts, axis=0, keepdims=True)
    return jnp.sum(wts[..., None] * o_all, axis=0)


def token_mixer(h, w_in, w_out, lq1, lk1, lq2, lk2, subln_gain, rel_bias, layer):
    B, S, _ = h.shape
    proj = h @ w_in
    qa, ka, va, qb, kb, vb = jnp.split(
        proj, [DIFF_WIDTH, 2 * DIFF_WIDTH, 3 * DIFF_WIDTH,
               3 * DIFF_WIDTH + DIL_WIDTH, 3 * DIFF_WIDTH + 2 * DIL_WIDTH], axis=-1)
    qa = qa.reshape(B, S, N_DIFF_HEADS, 2, HEAD_DIM).transpose(0, 2, 3, 1, 4)
    ka = ka.reshape(B, S, N_DIFF_HEADS, 2, HEAD_DIM).transpose(0, 2, 3, 1, 4)
    va = va.reshape(B, S, N_DIFF_HEADS, 2 * HEAD_DIM).transpose(0, 2, 1, 3)
    lam0 = lambda_init(layer)
    f32 = jnp.float32
    lam = (jnp.exp(jnp.sum(lq1.astype(f32) * lk1.astype(f32)))
           - jnp.exp(jnp.sum(lq2.astype(f32) * lk2.astype(f32))) + lam0)
    oa = diff_attention(qa, ka, va, lam, rel_bias[:, :2 * N_DIFF_HEADS])
    oa = rmsnorm(oa, subln_gain) * (1.0 - lam0)
    oa = oa.transpose(0, 2, 1, 3).reshape(B, S, DIFF_WIDTH)
    qb = qb.reshape(B, S, N_DIL_HEADS, HEAD_DIM).transpose(0, 2, 1, 3)
    kb = kb.reshape(B, S, N_DIL_HEADS, HEAD_DIM).transpose(0, 2, 1, 3)
    vb = vb.reshape(B, S, N_DIL_HEADS, HEAD_DIM).transpose(0, 2, 1, 3)
    ob = dilated_attention(qb, kb, vb, rel_bias[:, 2 * N_DIFF_HEADS:])
    ob = ob.transpose(0, 2, 1, 3).reshape(B, S, DIL_WIDTH)
    mixed = jnp.concatenate([oa, ob], axis=-1).astype(h.dtype)
    return mixed @ w_out


def setup_inputs(seed: int = 0) -> dict:
    key = jax.random.key(seed)
    ks = jax.random.split(key, 20)
    f32 = jnp.float32
    nrm = lambda k, shape, scale: jax.random.normal(k, shape, f32) * scale
    gain = lambda k, shape: 1.0 + 0.05 * jax.random.normal(k, shape, f32)
    return {
        'x': jax.random.normal(ks[0], (BATCH, SEQ, D_MODEL), f32),
        'ffn1_norm': gain(ks[1], (DEPTH, D_MODEL)),
        'ffn1_w_gate': nrm(ks[2], (DEPTH, D_MODEL, D_FF), D_MODEL ** -0.5),
        'ffn1_w_up': nrm(ks[3], (DEPTH, D_MODEL, D_FF), D_MODEL ** -0.5),
        'ffn1_w_down': nrm(ks[4], (DEPTH, D_FF, D_MODEL), D_FF ** -0.5),
        'mix_norm': gain(ks[5], (DEPTH, D_MODEL)),
        'w_in': nrm(ks[6], (DEPTH, D_MODEL, 3 * MIX_WIDTH), D_MODEL ** -0.5),
        'lambda_q1': nrm(ks[7], (DEPTH, HEAD_DIM), 0.1),
        'lambda_k1': nrm(ks[8], (DEPTH, HEAD_DIM), 0.1),
        'lambda_q2': nrm(ks[9], (DEPTH, HEAD_DIM), 0.1),
        'lambda_k2': nrm(ks[10], (DEPTH, HEAD_DIM), 0.1),
        'subln_gain': gain(ks[11], (DEPTH, 2 * HEAD_DIM)),
        'w_out': nrm(ks[12], (DEPTH, MIX_WIDTH, D_MODEL), MIX_WIDTH ** -0.5),
        'ffn2_norm': gain(ks[13], (DEPTH, D_MODEL)),
        'ffn2_w_gate': nrm(ks[14], (DEPTH, D_MODEL, D_FF), D_MODEL ** -0.5),
        'ffn2_w_up': nrm(ks[15], (DEPTH, D_MODEL, D_FF), D_MODEL ** -0.5),
        'ffn2_w_down': nrm(ks[16], (DEPTH, D_FF, D_MODEL), D_FF ** -0.5),
        'rel_bias': nrm(ks[17], (NUM_BUCKETS, N_BIAS_COLS), 0.3),
        'final_norm': gain(ks[18], (D_MODEL,)),
    }


def reference(x, ffn1_norm, ffn1_w_gate, ffn1_w_up, ffn1_w_down, mix_norm, w_in,
              lambda_q1, lambda_k1, lambda_q2, lambda_k2, subln_gain, w_out,
              ffn2_norm, ffn2_w_gate, ffn2_w_up, ffn2_w_down, rel_bias, final_norm):
    for layer in range(DEPTH):
        x = x + 0.5 * swiglu(rmsnorm(x, ffn1_norm[layer]),
                             ffn1_w_gate[layer], ffn1_w_up[layer], ffn1_w_down[layer])
        x = x + token_mixer(rmsnorm(x, mix_norm[layer]), w_in[layer], w_out[layer],
                            lambda_q1[layer], lambda_k1[layer], lambda_q2[layer], lambda_k2[layer],
                            subln_gain[layer], rel_bias, layer)
        x = x + 0.5 * swiglu(rmsnorm(x, ffn2_norm[layer]),
                             ffn2_w_gate[layer], ffn2_w_up[layer], ffn2_w_down[layer])
    return rmsnorm(x, final_norm)
```

```python
import ml_dtypes
from concourse.bass_utils import run_bass_kernel_spmd
import contextlib
import concourse.bass as bass
import concourse.mybir as mybir

ENGS = ("pe", "act", "dve", "pool", "sp")
N_DMA_SEMS = {"sp": 12, "pool": 12, "act": 6}


class SemPool:
    def __init__(self, nc, stack):
        self.esem = {e: stack.enter_context(nc.semaphore("s_" + e)) for e in ENGS}
        self.dsem = {e: [stack.enter_context(nc.semaphore("d_%s%d" % (e, k))) for k in range(N_DMA_SEMS[e])]
                     for e in N_DMA_SEMS}
        self.ecnt = {e: 0 for e in ENGS}
        self.dcnt = {e: [0] * N_DMA_SEMS[e] for e in N_DMA_SEMS}
        self.dnext = {e: 0 for e in N_DMA_SEMS}
        self.stage = 0


class Sched:
    def __init__(self):
        self.ops = []
        self.res = {}

    def add(self, eng, fn, reads=(), writes=(), dma=False, extra_deps=()):
        i = len(self.ops)
        deps = set(extra_deps)
        for k in reads:
            r = self.res.get(k)
            if r is not None and r[0] is not None:
                deps.add(r[0])
        for k in writes:
            r = self.res.get(k)
            if r is not None:
                if r[1]:
                    deps.update(r[1])
                elif r[0] is not None:
                    deps.add(r[0])
        for k in reads:
            self.res.setdefault(k, [None, []])[1].append(i)
        for k in writes:
            self.res[k] = [i, []]
        deps.discard(i)
        best = {}
        keep = set()
        for d in deps:
            od = self.ops[d]
            if od["dma"]:
                keep.add(d)
            else:
                b = best.get(od["eng"])
                if b is None or d > b:
                    best[od["eng"]] = d
        keep.update(best.values())
        self.ops.append(dict(eng=eng, fn=fn, deps=keep, dma=dma, sig=dma))
        return i

    def emit(self, nc, sp):
        ops = self.ops
        base_e = dict(sp.ecnt)
        dma_hist = {e: {} for e in N_DMA_SEMS}
        for i, op in enumerate(ops):
            if op["dma"]:
                e = op["eng"]
                slot = sp.dnext[e]
                sp.dnext[e] = (slot + 1) % N_DMA_SEMS[e]
                op["slot"] = slot
                sp.dcnt[e][slot] += 16
                op["val"] = sp.dcnt[e][slot]
                if slot in dma_hist[e]:
                    op["deps"].add(dma_hist[e][slot])
                dma_hist[e][slot] = i
        for i, op in enumerate(ops):
            keep = set()
            for d in op["deps"]:
                od = ops[d]
                if (not od["dma"]) and od["eng"] == "pe" and op["eng"] == "pe" and not op["dma"]:
                    continue
                keep.add(d)
                od["sig"] = True
            op["deps"] = keep
        for e in ENGS:
            for op in reversed(ops):
                if op["eng"] == e and not op["dma"] and op["fn"] is not None:
                    op["sig"] = True
                    break
        for op in ops:
            if op["sig"] and not op["dma"]:
                sp.ecnt[op["eng"]] += 1
                op["val"] = sp.ecnt[op["eng"]]
        prev_e = base_e
        prev_d = {e: [sp.dcnt[e][k] for k in range(N_DMA_SEMS[e])] for e in N_DMA_SEMS}
        for op in ops:
            if op["dma"]:
                prev_d[op["eng"]][op["slot"]] -= 16

        def event(d):
            od = ops[d]
            if od["dma"]:
                return sp.dsem[od["eng"]][od["slot"]], od["val"], ("d", od["eng"], od["slot"])
            return sp.esem[od["eng"]], od["val"], ("e", od["eng"])

        first_stage = sp.stage == 0
        sp.stage += 1
        with nc.Block() as block:
            def run(eng_name, e):
                waited = {}
                if not first_stage:
                    for e2 in ENGS:
                        if prev_e[e2] > 0 and e2 != eng_name:
                            e.wait_ge(sp.esem[e2], prev_e[e2])
                        waited[("e", e2)] = prev_e[e2]
                    for e2 in N_DMA_SEMS:
                        for k in range(N_DMA_SEMS[e2]):
                            if prev_d[e2][k] > 0:
                                e.wait_ge(sp.dsem[e2][k], prev_d[e2][k])
                            waited[("d", e2, k)] = prev_d[e2][k]
                for op in ops:
                    if op["eng"] != eng_name:
                        continue
                    need = {}
                    for d in op["deps"]:
                        s, v, key = event(d)
                        if waited.get(key, 0) >= v:
                            continue
                        if key not in need or need[key][1] < v:
                            need[key] = (s, v)
                    for key, (s, v) in need.items():
                        e.wait_ge(s, v)
                        waited[key] = v
                    if op["fn"] is None:
                        continue
                    ins = op["fn"](e)
                    if op["dma"]:
                        ins.then_inc(sp.dsem[eng_name][op["slot"]], 16)
                    elif op["sig"]:
                        ins.then_inc(sp.esem[eng_name], 1)

            @block.tensor
            def _(e):
                run("pe", e)

            @block.scalar
            def _(e):
                run("act", e)

            @block.vector
            def _(e):
                run("dve", e)

            @block.gpsimd
            def _(e):
                run("pool", e)

            @block.sync
            def _(e):
                run("sp", e)


import numpy as np
import concourse.bass as bass
import concourse.mybir as mybir

F32 = mybir.dt.float32
BF16 = mybir.dt.bfloat16
AF = mybir.ActivationFunctionType
ALU = mybir.AluOpType

D = 1024
KC = 8
DFF = 2816
NJ = 22
T = 2048
TT = 512
NT = T // TT
EPS = 1e-6


def ts(i, n):
    return slice(i * n, (i + 1) * n)


class Ctx:
    pass


def emit_norm(S, c, gkey, G32, kcs=range(KC)):
    for tt in range(NT):
        bank = 6 + (tt % 2)
        for kc in range(KC):
            sb = (tt * KC + kc) % 2
            S.add("act", lambda e, kc=kc, tt=tt, sb=sb: e.activation(
                c.SQ[:, sb, :], c.XT[:, kc, ts(tt, TT)], AF.Square),
                reads=[("XT", kc, tt)], writes=[("SQ", sb)])
            S.add("pe", lambda e, kc=kc, sb=sb, bank=bank: e.matmul(
                c.PS[:, bank, :], c.ONES[:, :], c.SQ[:, sb, :], start=(kc == 0), stop=(kc == KC - 1)),
                reads=[("SQ", sb)], writes=[("PS", bank)])
        rb = tt % 2
        S.add("dve", lambda e, bank=bank, rb=rb: e.tensor_scalar(
            c.RSTD[:, rb, :], c.PS[:, bank, :], 1.0 / D, EPS, ALU.mult, ALU.add),
            reads=[("PS", bank)], writes=[("RSTD", rb)])
        S.add("dve", lambda e, rb=rb: e.reciprocal(c.RSTD[:, rb, :], c.RSTD[:, rb, :]),
            reads=[("RSTD", rb)], writes=[("RSTD", rb)])
        S.add("act", lambda e, rb=rb: e.activation(c.RSTD[:, rb, :], c.RSTD[:, rb, :], AF.Sqrt),
            reads=[("RSTD", rb)], writes=[("RSTD", rb)])
        for kc in range(KC):
            eng = "dve"
            S.add(eng, lambda e, kc=kc, tt=tt, rb=rb: e.scalar_tensor_tensor(
                c.HT[:, kc, ts(tt, TT)], c.XT[:, kc, ts(tt, TT)], G32[:, kc:kc + 1], c.RSTD[:, rb, :],
                ALU.mult, ALU.mult),
                reads=[("XT", kc, tt), ("RSTD", rb), gkey], writes=[("HT", kc, tt)])


def emit_ffn(S, c, wg, wu, wd):
    for j in range(NJ):
        wb = j % 2
        S.add("pool", lambda e, j=j, wb=wb: e.dma_start(out=c.WG[wb], in_=wg[j]),
              writes=[("WG", wb)], dma=True)
        S.add("pool", lambda e, j=j, wb=wb: e.dma_start(out=c.WU[wb], in_=wu[j]),
              writes=[("WU", wb)], dma=True)
        for tt in range(NT):
            pb = (j * NT + tt) % 2
            bg, bu = pb, 2 + pb
            for kc in range(KC):
                S.add("pe", lambda e, kc=kc, tt=tt, wb=wb, bg=bg: e.matmul(
                    c.PS[:, bg, :], c.WG[wb][:, kc, :], c.HT[:, kc, ts(tt, TT)],
                    start=(kc == 0), stop=(kc == KC - 1)),
                    reads=[("WG", wb), ("HT", kc, tt)], writes=[("PS", bg)])
            for kc in range(KC):
                S.add("pe", lambda e, kc=kc, tt=tt, wb=wb, bu=bu: e.matmul(
                    c.PS[:, bu, :], c.WU[wb][:, kc, :], c.HT[:, kc, ts(tt, TT)],
                    start=(kc == 0), stop=(kc == KC - 1)),
                    reads=[("WU", wb), ("HT", kc, tt)], writes=[("PS", bu)])
            S.add("act", lambda e, bg=bg, pb=pb: e.activation(c.SIL[:, pb, :], c.PS[:, bg, :], AF.Silu),
                  reads=[("PS", bg)], writes=[("SIL", pb)])
            S.add("dve", lambda e, j=j, tt=tt, bu=bu, pb=pb: e.tensor_tensor(
                c.ACTT[:, j, ts(tt, TT)], c.SIL[:, pb, :], c.PS[:, bu, :], ALU.mult),
                reads=[("SIL", pb), ("PS", bu)], writes=[("ACTT", j, tt)])
    ALIAS = [("WG", 0), ("WG", 1), ("WU", 0)]
    for cc in range(KC):
        wb = cc % 2
        wkeys = [("WD", 0)] if wb == 0 else ALIAS
        S.add("pool", lambda e, cc=cc, wb=wb: e.dma_start(out=c.WD[wb], in_=wd[cc]),
              writes=wkeys, dma=True)
        for j in range(NJ):
            for tt in range(NT):
                bank = 4 * wb + tt
                S.add("pe", lambda e, j=j, tt=tt, wb=wb, bank=bank: e.matmul(
                    c.PS[:, bank, :], c.WD[wb][:, j, :], c.ACTT[:, j, ts(tt, TT)],
                    start=(j == 0), stop=(j == NJ - 1)),
                    reads=wkeys + [("ACTT", j, tt)], writes=[("PS", bank)])
        for tt in range(NT):
            bank = 4 * wb + tt
            S.add("dve", lambda e, cc=cc, tt=tt, bank=bank: e.scalar_tensor_tensor(
                c.XT[:, cc, ts(tt, TT)], c.PS[:, bank, :], 0.5, c.XT[:, cc, ts(tt, TT)],
                ALU.mult, ALU.add),
                reads=[("PS", bank), ("XT", cc, tt)], writes=[("XT", cc, tt)])


import math, contextlib, os
NSTAGE = int(os.environ.get('NSTAGE', '3'))
D3 = int(os.environ.get('D3', '9'))
import numpy as np
import concourse.bass as bass
import concourse.mybir as mybir

S_LEN = 4096
NB = 32
NQC = 8
WCOLS = 1536
STRIP = 2560
DSTRIP = 3072
DILS = (1, 4, 16)
NEG = -30000.0


def t5_bucket_np(dist):
    n = np.maximum(dist, 0)
    nf = np.maximum(n, 1).astype(np.float32)
    large = 16 + (np.log(nf / np.float32(16)) / np.float32(math.log(2048 / 16)) * np.float32(16)).astype(np.int32)
    large = np.minimum(large, 31)
    return np.where(n < 16, n, large)


def prep_B_consts(rel_bias, g):
    p = np.arange(128)[:, None]
    c = np.arange(STRIP)[None, :]
    dd = c - p - 384
    bk = t5_bucket_np(dd)
    strips = np.empty((128, 4, STRIP), np.float32)
    for hh in range(2):
        for m in range(2):
            col = (2 * g + hh) * 2 + m
            strips[:, hh * 2 + m, :] = np.where(dd >= 0, rel_bias[bk, col], np.float32(NEG))
    c = np.arange(DSTRIP)[None, :]
    dd = c - p - 384
    bk = t5_bucket_np(dd)
    dtile = np.empty((128, 5, DSTRIP), np.float32)
    for h in range(4):
        dtile[:, h, :] = rel_bias[bk, 8 + 4 * g + h]
    ddc = np.maximum(dd, 0)
    cnt = ((ddc <= 128).astype(np.float32) + ((ddc <= 512) & (ddc % 4 == 0)).astype(np.float32)
           + ((ddc <= 2048) & (ddc % 16 == 0)).astype(np.float32))
    dtile[:, 4, :] = np.where(dd >= 0, cnt, 0.0)
    return strips, dtile


def prep_B_weights(w_in_l, g):
    cols = []
    for base in (0, 512, 1024, 1536, 2048, 2560):
        cols.append(w_in_l[:, base + g * 256: base + (g + 1) * 256])
    w = np.concatenate(cols, axis=1)
    return np.ascontiguousarray(w.reshape(8, 128, WCOLS).transpose(1, 0, 2))


def build_B(layer):
    lam0 = 0.8 - 0.6 * math.exp(-0.3 * layer)
    nc = bass.Bass("TRN2", target_bir_lowering=False)
    hT = nc.dram_tensor("hT", [2, 1024, 2048], BF16, kind="ExternalInput").ap()
    w = nc.dram_tensor("w", [128, 8, WCOLS], F32, kind="ExternalInput").ap()
    strips = nc.dram_tensor("strips", [128, 4, STRIP], F32, kind="ExternalInput").ap()
    dtile = nc.dram_tensor("dtile", [128, 5, DSTRIP], F32, kind="ExternalInput").ap()
    lamv = nc.dram_tensor("lamv", [128, 4, 64], F32, kind="ExternalInput").ap()
    sgain = nc.dram_tensor("sgain", [128, 128], F32, kind="ExternalInput").ap()
    ident = nc.dram_tensor("ident", [128, 128], F32, kind="ExternalInput").ap()
    mixT = nc.dram_tensor("mixT", [512, S_LEN], BF16, kind="ExternalOutput").ap()
    emit_B(nc, lam0, hT, w, strips, dtile, lamv, sgain, ident, mixT)
    return nc


def emit_B(nc, lam0, hT, w, strips, dtile, lamv, sgain, ident, mixT, sp=None, final_wait=True):
    with contextlib.ExitStack() as st:
        def sb(name, shape, dt):
            return st.enter_context(nc.sbuf_tensor(name, shape, dt))
        if sp is None:
            sp = SemPool(nc, st)
        PS = st.enter_context(nc.psum_tensor("PSB", [128, 8, 512], F32))
        QBT = sb("QBT", [128, 2, S_LEN], BF16)
        KBT = sb("KBT", [128, 2, S_LEN], BF16)
        VB = sb("VB", [128, 3, NB, 4, 65], BF16)
        IDN = sb("IDN", [128, 128], BF16)
        ONE32 = sb("ONE32", [128, 64], F32)
        LAM = sb("LAM", [128, 8], F32)
        SG = sb("SG", [128, 128], F32)
        sd = contextlib.ExitStack()
        QAT = sd.enter_context(nc.sbuf_tensor("QAT", [128, 2, S_LEN], BF16))
        KAT = sd.enter_context(nc.sbuf_tensor("KAT", [128, 2, S_LEN], BF16))
        VA = sd.enter_context(nc.sbuf_tensor("VA", [128, NB, 2, 129], BF16))
        with contextlib.ExitStack() as s1:
            def sb1(name, shape, dt):
                return s1.enter_context(nc.sbuf_tensor(name, shape, dt))
            S = Sched()
            HTs = sb1("HTs", [128, 8, 2048], BF16)
            W = sb1("W", [128, 8, WCOLS], BF16)
            LV = sb1("LV", [128, 4, 64], F32)
            HTP = sb1("HTP", [128, 2, 8, 512], BF16)
            hpc = [0]
            S.add("pool", lambda e: e.dma_start(out=W[:, :, :], in_=w), writes=[("W",)], dma=True)
            S.add("pool", lambda e: e.dma_start(out=IDN[:, :], in_=ident), writes=[("IDN",)], dma=True)
            S.add("sp", lambda e: e.dma_start(out=LV[:, :, :], in_=lamv), writes=[("LV",)], dma=True)
            S.add("sp", lambda e: e.dma_start(out=SG[:, :], in_=sgain), writes=[("SG",)], dma=True)
            S.add("dve", lambda e: e.memset(ONE32[:, :], 1.0), writes=[("ONE32",)])
            S.add("pool", lambda e: e.memset(VA[:, :, :, 128:129], 1.0), writes=[("VAones",)])
            S.add("pool", lambda e: e.memset(VB[:, :, :, :, 64:65], 1.0), writes=[("VBones",)])
            S.add("dve", lambda e: e.tensor_tensor(LV[:, 0, :], LV[:, 0, :], LV[:, 1, :], ALU.mult),
                  reads=[("LV",)], writes=[("LV",)])
            S.add("dve", lambda e: e.tensor_tensor(LV[:, 2, :], LV[:, 2, :], LV[:, 3, :], ALU.mult),
                  reads=[("LV",)], writes=[("LV",)])
            S.add("dve", lambda e: e.tensor_reduce(LAM[:, 2:3], LV[:, 0, :], mybir.AxisListType.X, ALU.add),
                  reads=[("LV",)], writes=[("LAM",)])
            S.add("dve", lambda e: e.tensor_reduce(LAM[:, 3:4], LV[:, 2, :], mybir.AxisListType.X, ALU.add),
                  reads=[("LV",)], writes=[("LAM",)])
            S.add("act", lambda e: e.activation(LAM[:, 4:6], LAM[:, 2:4], AF.Exp), reads=[("LAM",)], writes=[("LAM",)])
            S.add("dve", lambda e: e.tensor_tensor(LAM[:, 0:1], LAM[:, 4:5], LAM[:, 5:6], ALU.subtract),
                  reads=[("LAM",)], writes=[("LAM",)])
            S.add("dve", lambda e: e.tensor_scalar(LAM[:, 0:1], LAM[:, 0:1], float(lam0), None, ALU.add),
                  reads=[("LAM",)], writes=[("LAM",)])
            S.add("dve", lambda e: e.tensor_scalar(LAM[:, 1:2], LAM[:, 0:1], -1.0, None, ALU.mult),
                  reads=[("LAM",)], writes=[("LAM",)])
            S.add("dve", lambda e: e.tensor_scalar(SG[:, :], SG[:, :], float(1.0 - lam0), None, ALU.mult),
                  reads=[("SG",)], writes=[("SG",)])
            cp = [0]

            def evac(out_ap, in_ap, reads, writes):
                eng = "act" if cp[0] % 2 == 0 else "dve"
                cp[0] += 1
                if eng == "act":
                    S.add("act", lambda e: e.copy(out_ap, in_ap), reads=reads, writes=writes)
                else:
                    S.add("dve", lambda e: e.tensor_copy(out_ap, in_ap), reads=reads, writes=writes)

            pbank = [0]

            def nextbank():
                b = pbank[0]
                pbank[0] = (b + 1) % 8
                return b

            for hf in range(2):
                for kc in range(8):
                    S.add("sp", lambda e, kc=kc, hf=hf: e.dma_start(out=HTs[:, kc, :], in_=hT[hf, ts(kc, 128), :]),
                          writes=[("HTs", kc)], dma=True)
                hkeys = [("HTs", kc) for kc in range(8)]
                for tt in range(4):
                    qc = hf * 4 + tt
                    for (dst, di, c0) in ((QAT, 0, 0), (QAT, 1, 128), (KAT, 0, 256), (KAT, 1, 384),
                                          (QBT, 0, 768), (QBT, 1, 896), (KBT, 0, 1024), (KBT, 1, 1152)):
                        b = nextbank()
                        for kc in range(8):
                            S.add("pe", lambda e, kc=kc, tt=tt, c0=c0, b=b: e.matmul(
                                PS[:, b, :], W[:, kc, c0:c0 + 128], HTs[:, kc, ts(tt, 512)],
                                start=(kc == 0), stop=(kc == 7)),
                                reads=[("W",), ("HTs", kc)], writes=[("PS", b)])
                        evac(dst[:, di, ts(qc, 512)], PS[:, b, :], [("PS", b)], [("QK", id(dst), di, qc)])
                    for bl in range(4):
                        blk = qc * 4 + bl
                        b = nextbank()
                        for kc in range(8):
                            S.add("pe", lambda e, kc=kc, tt=tt, bl=bl, b=b: e.matmul(
                                PS[:, b, 0:256], HTs[:, kc, tt * 512 + bl * 128: tt * 512 + (bl + 1) * 128],
                                W[:, kc, 512:768], start=(kc == 0), stop=(kc == 7)),
                                reads=[("W",), ("HTs", kc)], writes=[("PS", b)])
                        evac(VA[:, blk, :, 0:128], PS[:, b, 0:256].rearrange("p (h d) -> p h d", h=2),
                             [("PS", b), ("VAones",)], [("VA", blk)])
                        b = nextbank()
                        for kc in range(8):
                            S.add("pe", lambda e, kc=kc, tt=tt, bl=bl, b=b: e.matmul(
                                PS[:, b, 0:256], HTs[:, kc, tt * 512 + bl * 128: tt * 512 + (bl + 1) * 128],
                                W[:, kc, 1280:1536], start=(kc == 0), stop=(kc == 7)),
                                reads=[("W",), ("HTs", kc)], writes=[("PS", b)])
                        evac(VB[:, 0, blk, :, 0:64], PS[:, b, 0:256].rearrange("p (h d) -> p h d", h=4),
                             [("PS", b), ("VBones",)], [("VB", 0, blk)])
                    hb = hpc[0] % 2
                    hpc[0] += 1
                    S.add("pool", lambda e, tt=tt, hb=hb: e.tensor_copy(
                        HTP[:, hb, :, :].rearrange("p k (r a) -> p k r a", r=4),
                        HTs[:, :, ts(tt, 512)].rearrange("p k (a r) -> p k r a", r=4)),
                        reads=hkeys, writes=[("HTP", hb)])
                    for r in range(4):
                        pblk = r * 8 + qc
                        b = nextbank()
                        for kc in range(8):
                            S.add("pe", lambda e, kc=kc, hb=hb, r=r, b=b: e.matmul(
                                PS[:, b, 0:256], HTP[:, hb, kc, ts(r, 128)],
                                W[:, kc, 1280:1536], start=(kc == 0), stop=(kc == 7)),
                                reads=[("W",), ("HTP", hb)], writes=[("PS", b)])
                        evac(VB[:, 1, pblk, :, 0:64], PS[:, b, 0:256].rearrange("p (h d) -> p h d", h=4),
                             [("PS", b), ("VBones",)], [("VB", 1, pblk)])
                for gq in range(4):
                    hb = hpc[0] % 2
                    hpc[0] += 1
                    S.add("pool", lambda e, gq=gq, hb=hb: e.tensor_copy(
                        HTP[:, hb, :, :].rearrange("p k (r a) -> p k r a", r=4),
                        HTs[:, :, :].rearrange("p k (a r) -> p k r a", r=16)[:, :, 4 * gq:4 * gq + 4, :]),
                        reads=hkeys, writes=[("HTP", hb)])
                    for r4 in range(4):
                        r = 4 * gq + r4
                        pblk = r * 2 + hf
                        b = nextbank()
                        for kc in range(8):
                            S.add("pe", lambda e, kc=kc, hb=hb, r4=r4, b=b: e.matmul(
                                PS[:, b, 0:256], HTP[:, hb, kc, ts(r4, 128)],
                                W[:, kc, 1280:1536], start=(kc == 0), stop=(kc == 7)),
                                reads=[("W",), ("HTP", hb)], writes=[("PS", b)])
                        evac(VB[:, 2, pblk, :, 0:64], PS[:, b, 0:256].rearrange("p (h d) -> p h d", h=4),
                             [("PS", b), ("VBones",)], [("VB", 2, pblk)])
            S.emit(nc, sp)
        outs = []
        if NSTAGE < 2:
            sd.close()
            return
        with contextlib.ExitStack() as s2:
            def sb2(name, shape, dt):
                return s2.enter_context(nc.sbuf_tensor(name, shape, dt))
            S = Sched()
            PT = sb2("PT", [128, 4, 512], BF16)
            EBS = sb2("EBS", [128, 4, STRIP], BF16)
            STG = sb2("STG2", [128, 2, STRIP], F32)
            for m in range(4):
                S.add("sp", lambda e, m=m: e.dma_start(out=STG[:, m % 2, :], in_=strips[:, m, :]),
                      writes=[("STG", m % 2)], dma=True)
                S.add("act", lambda e, m=m: e.activation(EBS[:, m, :], STG[:, m % 2, :], AF.Exp),
                      reads=[("STG", m % 2)], writes=[("EBS", m)])
            OA = sb2("OA", [128, 2, 128], F32)
            OAN = sb2("OAN", [128, 2, 128], BF16)
            SQJ = sb2("SQJ", [128, 128], F32)
            RC = sb2("RC", [128, 8, 4], F32)
            MIXA = sb2("MIXA", [128, 2, S_LEN], BF16)
            it = 0
            fin = 0
            for hh in range(2):
                for qc in range(NQC):
                    nkb = 4 * qc + 4
                    for m in range(2):
                        for kb in range(nkb):
                            sbk = it % 2
                            pt = it % 4
                            it += 1
                            delta = 512 * qc - 128 * kb
                            c0 = min(delta + 384, 2048)
                            S.add("pe", lambda e, hh=hh, m=m, kb=kb, qc=qc, sbk=sbk: e.matmul(
                                PS[:, sbk, :], KAT[m * 64:(m + 1) * 64, hh, ts(kb, 128)],
                                QAT[m * 64:(m + 1) * 64, hh, ts(qc, 512)], start=True, stop=True),
                                writes=[("PS", sbk)])
                            S.add("act", lambda e, sbk=sbk, pt=pt: e.activation(
                                PT[:, pt, :], PS[:, sbk, :], AF.Exp, scale=0.125),
                                reads=[("PS", sbk)], writes=[("PT", pt)])
                            S.add("dve", lambda e, pt=pt, hh=hh, m=m, c0=c0: e.tensor_tensor(
                                PT[:, pt, :], PT[:, pt, :], EBS[:, hh * 2 + m, c0:c0 + 512], ALU.mult),
                                reads=[("PT", pt), ("EBS", hh * 2 + m)], writes=[("PT", pt)])
                            for sbi in range(4):
                                qi = 4 * qc + sbi
                                if kb > qi:
                                    continue
                                ob = 2 + 2 * m + sbi // 2
                                oc = (sbi % 2) * 129
                                S.add("pe", lambda e, pt=pt, sbi=sbi, kb=kb, hh=hh, ob=ob, oc=oc, qi=qi: e.matmul(
                                    PS[:, ob, oc:oc + 129], PT[:, pt, ts(sbi, 128)], VA[:, kb, hh, :],
                                    start=(kb == 0 and sbi % 2 == 0), stop=(kb == qi)),
                                    reads=[("PT", pt)], writes=[("PSO", m, sbi)])
                    for sbi in range(4):
                        qi = 4 * qc + sbi
                        f = fin % 2
                        rc = fin % 8
                        fin += 1
                        o0 = PS[:, 2 + sbi // 2, (sbi % 2) * 129:(sbi % 2) * 129 + 129]
                        o1 = PS[:, 4 + sbi // 2, (sbi % 2) * 129:(sbi % 2) * 129 + 129]
                        S.add("dve", lambda e, o0=o0, rc=rc: e.reciprocal(RC[:, rc, 0:1], o0[:, 128:129]),
                              reads=[("PSO", 0, sbi)], writes=[("RC", rc)])
                        S.add("dve", lambda e, o1=o1, rc=rc: e.reciprocal(RC[:, rc, 1:2], o1[:, 128:129]),
                              reads=[("PSO", 1, sbi)], writes=[("RC", rc)])
                        S.add("dve", lambda e, rc=rc: e.tensor_tensor(RC[:, rc, 1:2], RC[:, rc, 1:2], LAM[:, 1:2], ALU.mult),
                              reads=[("RC", rc)], writes=[("RC", rc)])
                        S.add("dve", lambda e, o0=o0, rc=rc, f=f: e.tensor_scalar(
                            OA[:, f, :], o0[:, 0:128], RC[:, rc, 0:1], None, ALU.mult),
                            reads=[("PSO", 0, sbi), ("RC", rc)], writes=[("OA", f)])
                        S.add("dve", lambda e, o1=o1, rc=rc, f=f: e.scalar_tensor_tensor(
                            OA[:, f, :], o1[:, 0:128], RC[:, rc, 1:2], OA[:, f, :], ALU.mult, ALU.add),
                            reads=[("PSO", 1, sbi), ("RC", rc), ("OA", f)], writes=[("OA", f), ("PSO", 0, sbi), ("PSO", 1, sbi)])
                        S.add("dve", lambda e, f=f: e.tensor_tensor(SQJ[:, :], OA[:, f, :], OA[:, f, :], ALU.mult),
                              reads=[("OA", f)], writes=[("SQJ",)])
                        S.add("dve", lambda e, rc=rc: e.tensor_reduce(RC[:, rc, 2:3], SQJ[:, :], mybir.AxisListType.X, ALU.add),
                              reads=[("SQJ",)], writes=[("RC", rc)])
                        S.add("dve", lambda e, rc=rc: e.tensor_scalar(
                            RC[:, rc, 2:3], RC[:, rc, 2:3], 1.0 / 128, EPS, ALU.mult, ALU.add),
                            reads=[("RC", rc)], writes=[("RC", rc)])
                        S.add("dve", lambda e, rc=rc: e.reciprocal(RC[:, rc, 2:3], RC[:, rc, 2:3]),
                              reads=[("RC", rc)], writes=[("RC", rc)])
                        S.add("act", lambda e, rc=rc: e.activation(RC[:, rc, 2:3], RC[:, rc, 2:3], AF.Sqrt),
                              reads=[("RC", rc)], writes=[("RC", rc)])
                        S.add("dve", lambda e, rc=rc, f=f: e.scalar_tensor_tensor(
                            OAN[:, f, :], OA[:, f, :], RC[:, rc, 2:3], SG[:, :], ALU.mult, ALU.mult),
                            reads=[("OA", f), ("RC", rc)], writes=[("OAN", f)])
                        tb = 6 + (fin % 2)
                        S.add("pe", lambda e, f=f, tb=tb: e.transpose(
                            PS[:, tb, :].bitcast(BF16)[:, 0:128], OAN[:, f, :], IDN[:, :]),
                            reads=[("OAN", f)], writes=[("PS", tb)])
                        S.add("act", lambda e, tb=tb, hh=hh, qi=qi: e.copy(
                            MIXA[:, hh, ts(qi, 128)], PS[:, tb, :].bitcast(BF16)[:, 0:128]),
                            reads=[("PS", tb)], writes=[("MIXA", hh, qc)])
                    if qc == NQC - 1:
                        pass
                outs.append(S.add("sp", lambda e, hh=hh: e.dma_start(out=mixT[ts(hh, 128), :], in_=MIXA[:, hh, :]),
                                  reads=[("MIXA", hh, q) for q in range(NQC)], dma=True))
            S.add("sp", None, extra_deps=outs)
            S.emit(nc, sp)
        sd.close()
        if NSTAGE < 3:
            return
        with contextlib.ExitStack() as s3:
            def sb3(name, shape, dt):
                return s3.enter_context(nc.sbuf_tensor(name, shape, dt))
            S = Sched()
            EBM = sb3("EBM", [128, 4, DSTRIP], BF16)
            STG = sb3("STG3", [128, 2, DSTRIP], F32)
            CNT = sb3("CNT", [128, DSTRIP], F32)
            PT = sb3("PT3", [128, 4, 512], BF16)
            OB = sb3("OB", [128, 2, 128], BF16)
            RC = sb3("RC3", [128, 8, 2], F32)
            MIXB = sb3("MIXB", [128, 2, S_LEN], BF16)
            S.add("sp", lambda e: e.dma_start(out=CNT[:, :], in_=dtile[:, 4, :]), writes=[("CNT",)], dma=True)
            for h in range(4):
                S.add("sp", lambda e, h=h: e.dma_start(out=STG[:, h % 2, :], in_=dtile[:, h, :]),
                      writes=[("STG", h % 2)], dma=True)
                S.add("act", lambda e, h=h: e.activation(STG[:, h % 2, :], STG[:, h % 2, :], AF.Exp),
                      reads=[("STG", h % 2)], writes=[("STG", h % 2)])
                S.add("dve", lambda e, h=h: e.tensor_tensor(EBM[:, h, :], STG[:, h % 2, :], CNT[:, :], ALU.mult),
                      reads=[("STG", h % 2), ("CNT",)], writes=[("EBM", h)])
            it = 0
            fin = 0
            outs = []
            for pair in range(2):
                for qc in range(NQC):
                    kb0 = max(0, 4 * qc - 16)
                    for hp in range(2):
                        h = 2 * pair + hp
                        ob = 2 + hp + 2 * (qc % 2)
                        for kb in range(kb0, 4 * qc + 4):
                            sbk = it % 2
                            pt = it % 4
                            it += 1
                            c0 = 512 * qc - 128 * kb + 384
                            S.add("pe", lambda e, hp=hp, pair=pair, kb=kb, qc=qc, sbk=sbk: e.matmul(
                                PS[:, sbk, :], KBT[hp * 64:(hp + 1) * 64, pair, ts(kb, 128)],
                                QBT[hp * 64:(hp + 1) * 64, pair, ts(qc, 512)], start=True, stop=True),
                                writes=[("PS", sbk)])
                            S.add("act", lambda e, sbk=sbk, pt=pt: e.activation(
                                PT[:, pt, :], PS[:, sbk, :], AF.Exp, scale=0.125),
                                reads=[("PS", sbk)], writes=[("PT", pt)])
                            S.add("dve", lambda e, pt=pt, h=h, c0=c0: e.tensor_tensor(
                                PT[:, pt, :], PT[:, pt, :], EBM[:, h, c0:c0 + 512], ALU.mult),
                                reads=[("PT", pt), ("EBM", h)], writes=[("PT", pt)])
                            for sbi in range(4):
                                qi = 4 * qc + sbi
                                if kb > qi:
                                    continue
                                S.add("pe", lambda e, pt=pt, sbi=sbi, kb=kb, h=h, ob=ob, qi=qi, kb0=kb0: e.matmul(
                                    PS[:, ob, sbi * 65:sbi * 65 + 65], PT[:, pt, ts(sbi, 128)], VB[:, 0, kb, h, :],
                                    start=(kb == kb0 and sbi == 0), stop=(kb == qi)),
                                    reads=[("PT", pt)], writes=[("PSO", ob)])
                    for sbi in range(4):
                        qi = 4 * qc + sbi
                        f = fin % 2
                        rc = fin % 8
                        fin += 1
                        for hp in range(2):
                            ob = 2 + hp + 2 * (qc % 2)
                            o = PS[:, ob, sbi * 65:sbi * 65 + 65]
                            S.add("dve", lambda e, o=o, rc=rc, hp=hp: e.reciprocal(RC[:, rc, hp:hp + 1], o[:, 64:65]),
                                  reads=[("PSO", ob)], writes=[("RC", rc)])
                            wr = [("OB", f)] + ([("PSO", ob)] if sbi == 3 else [])
                            S.add("dve", lambda e, o=o, rc=rc, hp=hp, f=f: e.tensor_scalar(
                                OB[:, f, hp * 64:(hp + 1) * 64], o[:, 0:64], RC[:, rc, hp:hp + 1], None, ALU.mult),
                                reads=[("PSO", ob), ("RC", rc)], writes=wr)
                        tb = 6 + (fin % 2)
                        S.add("pe", lambda e, f=f, tb=tb: e.transpose(
                            PS[:, tb, :].bitcast(BF16)[:, 0:128], OB[:, f, :], IDN[:, :]),
                            reads=[("OB", f)], writes=[("PS", tb)])
                        S.add("act", lambda e, tb=tb, pair=pair, qi=qi: e.copy(
                            MIXB[:, pair, ts(qi, 128)], PS[:, tb, :].bitcast(BF16)[:, 0:128]),
                            reads=[("PS", tb)], writes=[("MIXB", pair, qc)])
                outs.append(S.add("sp", lambda e, pair=pair: e.dma_start(
                    out=mixT[256 + pair * 128: 256 + (pair + 1) * 128, :], in_=MIXB[:, pair, :]),
                    reads=[("MIXB", pair, q) for q in range(NQC)], dma=True))
            if final_wait:
                S.add("sp", None, extra_deps=outs)
            S.emit(nc, sp)


def emit_wout(S, c, wo, MT):
    for cc in range(KC):
        wb = cc % 2
        S.add("pool", lambda e, cc=cc, wb=wb: e.dma_start(out=c.WG[wb], in_=wo[cc]), writes=[("WG", wb)], dma=True)
        for tt in range(NT):
            bank = 4 * wb + tt
            for kc in range(KC):
                S.add("pe", lambda e, kc=kc, tt=tt, wb=wb, bank=bank: e.matmul(
                    c.PS[:, bank, :], c.WG[wb][:, kc, :], c.HT[:, kc, ts(tt, TT)],
                    start=(kc == 0), stop=(kc == KC - 1)),
                    reads=[("WG", wb), ("HT", kc, tt)], writes=[("PS", bank)])
            S.add("dve", lambda e, cc=cc, tt=tt, bank=bank: e.tensor_tensor(
                c.XT[:, cc, ts(tt, TT)], c.XT[:, cc, ts(tt, TT)], c.PS[:, bank, :], ALU.add),
                reads=[("PS", bank), ("XT", cc, tt)], writes=[("XT", cc, tt)])


def emit_final_norm(S, c, gkey, G):
    for tt in range(NT):
        bank = 6 + (tt % 2)
        for kc in range(KC):
            sb = (tt * KC + kc) % 2
            S.add("act", lambda e, kc=kc, tt=tt, sb=sb: e.activation(
                c.SQ[:, sb, :], c.XT[:, kc, ts(tt, TT)], AF.Square),
                reads=[("XT", kc, tt)], writes=[("SQ", sb)])
            S.add("pe", lambda e, kc=kc, sb=sb, bank=bank: e.matmul(
                c.PS[:, bank, :], c.ONES[:, :], c.SQ[:, sb, :], start=(kc == 0), stop=(kc == KC - 1)),
                reads=[("SQ", sb)], writes=[("PS", bank)])
        rb = tt % 2
        S.add("dve", lambda e, bank=bank, rb=rb: e.tensor_scalar(
            c.RSTD[:, rb, :], c.PS[:, bank, :], 1.0 / D, EPS, ALU.mult, ALU.add),
            reads=[("PS", bank)], writes=[("RSTD", rb)])
        S.add("dve", lambda e, rb=rb: e.reciprocal(c.RSTD[:, rb, :], c.RSTD[:, rb, :]),
              reads=[("RSTD", rb)], writes=[("RSTD", rb)])
        S.add("act", lambda e, rb=rb: e.activation(c.RSTD[:, rb, :], c.RSTD[:, rb, :], AF.Sqrt),
              reads=[("RSTD", rb)], writes=[("RSTD", rb)])
        for kc in range(KC):
            S.add("dve", lambda e, kc=kc, tt=tt, rb=rb: e.scalar_tensor_tensor(
                c.XT[:, kc, ts(tt, TT)], c.XT[:, kc, ts(tt, TT)], G[:, kc:kc + 1], c.RSTD[:, rb, :],
                ALU.mult, ALU.mult),
                reads=[("XT", kc, tt), ("RSTD", rb), gkey], writes=[("XT", kc, tt)])


def build_T(kind):
    nc = bass.Bass("TRN2", target_bir_lowering=False)
    def din(name, shape, dt=F32):
        return nc.dram_tensor(name, shape, dt, kind="ExternalInput").ap()
    xT = din("xT", [D, T])
    gains = din("gains", [128, 4, KC])
    if kind != "A0":
        mT = din("mT", [D, T], BF16)
        wo = din("wo", [KC, 128, KC, 128])
        wgB, wuB, wdB = din("wgB", [NJ, 128, KC, 128]), din("wuB", [NJ, 128, KC, 128]), din("wdB", [KC, 128, NJ, 128])
    if kind != "C1":
        wgA, wuA, wdA = din("wgA", [NJ, 128, KC, 128]), din("wuA", [NJ, 128, KC, 128]), din("wdA", [KC, 128, NJ, 128])
        hT_o = nc.dram_tensor("hT_o", [D, T], BF16, kind="ExternalOutput").ap()
    xT_o = nc.dram_tensor("xT_o", [D, T], F32, kind="ExternalOutput").ap()
    S = Sched()
    c = Ctx()
    with contextlib.ExitStack() as st:
        def sb(name, shape, dt):
            return st.enter_context(nc.sbuf_tensor(name, shape, dt))
        c.XT = sb("XT", [128, KC, T], F32)
        c.HT = sb("HT", [128, KC, T], BF16)
        c.ACTT = sb("ACTT", [128, NJ, T], BF16)
        SCR = sb("SCR", [128, 4096], BF16)
        c.WG = [SCR[:, 0:1024].rearrange("p (k n) -> p k n", k=KC), SCR[:, 1024:2048].rearrange("p (k n) -> p k n", k=KC)]
        c.WU = [SCR[:, 2048:3072].rearrange("p (k n) -> p k n", k=KC), SCR[:, 3072:4096].rearrange("p (k n) -> p k n", k=KC)]
        WD0 = sb("WD0", [128, NJ, 128], BF16)
        c.WD = [WD0[:, :, :], SCR[:, 0:NJ * 128].rearrange("p (j n) -> p j n", j=NJ)]
        c.SQ = sb("SQ", [128, 2, TT], BF16)
        c.RSTD = sb("RSTD", [128, 2, TT], F32)
        c.SIL = sb("SIL", [128, 2, TT], BF16)
        c.ONES = sb("ONES", [128, 128], BF16)
        G = sb("G", [128, 4, KC], F32)
        c.PS = st.enter_context(nc.psum_tensor("PS", [128, 8, TT], F32))
        S.add("dve", lambda e: e.memset(c.ONES[:, :], 1.0), writes=[("ONES",)])
        S.add("sp", lambda e: e.dma_start(out=G[:, :, :], in_=gains), writes=[("G",)], dma=True)
        for kc in range(KC):
            S.add("sp", lambda e, kc=kc: e.dma_start(out=c.XT[:, kc, :], in_=xT[ts(kc, 128), :]),
                  writes=[("XT", kc, tt) for tt in range(NT)], dma=True)
        gi = 0
        if kind != "A0":
            for kc in range(KC):
                S.add("sp", lambda e, kc=kc: e.dma_start(out=c.HT[:, kc, :], in_=mT[ts(kc, 128), :]),
                      writes=[("HT", kc, tt) for tt in range(NT)], dma=True)
            emit_wout(S, c, wo, None)
            emit_norm(S, c, ("G",), G[:, gi, :]); gi += 1
            emit_ffn(S, c, wgB, wuB, wdB)
        outs = []
        if kind != "C1":
            emit_norm(S, c, ("G",), G[:, gi, :]); gi += 1
            emit_ffn(S, c, wgA, wuA, wdA)
            emit_norm(S, c, ("G",), G[:, gi, :]); gi += 1
            for kc in range(KC):
                outs.append(S.add("sp", lambda e, kc=kc: e.dma_start(out=hT_o[ts(kc, 128), :], in_=c.HT[:, kc, :]),
                                  reads=[("HT", kc, tt) for tt in range(NT)], dma=True))
        else:
            emit_final_norm(S, c, ("G",), G[:, gi, :]); gi += 1
        for kc in range(KC):
            outs.append(S.add("sp", lambda e, kc=kc: e.dma_start(out=xT_o[ts(kc, 128), :], in_=c.XT[:, kc, :]),
                              reads=[("XT", kc, tt) for tt in range(NT)], dma=True))
        S.add("sp", None, extra_deps=outs)
        S.emit(nc, SemPool(nc, st))
    return nc


def _ffn_layouts(Wg, Wu, Wd):
    wg_l = np.ascontiguousarray(Wg.reshape(KC, 128, NJ, 128).transpose(2, 1, 0, 3))
    wu_l = np.ascontiguousarray(Wu.reshape(KC, 128, NJ, 128).transpose(2, 1, 0, 3))
    wd_l = np.ascontiguousarray(Wd.reshape(NJ, 128, KC, 128).transpose(2, 1, 0, 3))
    return wg_l, wu_l, wd_l


def _gain_layout(vs):
    g = np.zeros((128, 4, KC), np.float32)
    for i, v in enumerate(vs):
        g[:, i, :] = np.asarray(v, np.float32).reshape(KC, 128).T
    return g


_NC_CACHE = {}


def _get(kind, *a):
    key = (kind,) + a
    if key not in _NC_CACHE:
        _NC_CACHE[key] = build_T(kind) if kind in ("A0", "CA", "C1") else build_B(*a)
    return _NC_CACHE[key]


def kernel(x, ffn1_norm, ffn1_w_gate, ffn1_w_up, ffn1_w_down, mix_norm, w_in,
           lambda_q1, lambda_k1, lambda_q2, lambda_k2, subln_gain, w_out,
           ffn2_norm, ffn2_w_gate, ffn2_w_up, ffn2_w_down, rel_bias, final_norm):
    f32 = np.float32
    x = np.asarray(x, f32)
    A = lambda v: np.asarray(v, f32)
    cores = list(range(8))
    xT = [np.ascontiguousarray(x[c // 2, (c % 2) * T:(c % 2 + 1) * T, :].T) for c in cores]
    ffn1 = [_ffn_layouts(A(ffn1_w_gate)[l], A(ffn1_w_up)[l], A(ffn1_w_down)[l]) for l in range(2)]
    ffn2 = [_ffn_layouts(A(ffn2_w_gate)[l], A(ffn2_w_up)[l], A(ffn2_w_down)[l]) for l in range(2)]
    wo_l = [np.ascontiguousarray(A(w_out)[l].reshape(KC, 128, KC, 128).transpose(2, 1, 0, 3)) for l in range(2)]
    ident = np.eye(128, dtype=f32)
    rb = A(rel_bias)

    def run_B(layer, hT_cores):
        ncB = _get("B", layer)
        lamv = np.ascontiguousarray(np.broadcast_to(
            np.stack([A(lambda_q1)[layer], A(lambda_k1)[layer], A(lambda_q2)[layer], A(lambda_k2)[layer]])[None],
            (128, 4, 64)))
        sgb = np.ascontiguousarray(np.broadcast_to(A(subln_gain)[layer][None], (128, 128)))
        maps = []
        for c in cores:
            b, g = c // 2, c % 2
            hT2 = np.ascontiguousarray(np.stack([hT_cores[2 * b], hT_cores[2 * b + 1]]))
            st_, dt_ = prep_B_consts(rb, g)
            maps.append(dict(hT=hT2, w=prep_B_weights(A(w_in)[layer], g), strips=st_, dtile=dt_, lamv=lamv,
                             sgain=sgb, ident=ident))
        res = run_bass_kernel_spmd(ncB, maps, core_ids=cores)
        mix = [np.asarray(r["mixT"]) for r in res.results]
        out = []
        for c in cores:
            b, hf = c // 2, c % 2
            m0, m1 = mix[2 * b], mix[2 * b + 1]
            sl = slice(hf * T, (hf + 1) * T)
            out.append(np.ascontiguousarray(np.concatenate(
                [m0[0:256, sl], m1[0:256, sl], m0[256:512, sl], m1[256:512, sl]], axis=0)))
        return out

    res = run_bass_kernel_spmd(_get("A0"), [dict(
        xT=xT[c], gains=_gain_layout([A(ffn1_norm)[0], A(mix_norm)[0]]),
        wgA=ffn1[0][0], wuA=ffn1[0][1], wdA=ffn1[0][2]) for c in cores], core_ids=cores)
    xT = [np.asarray(r["xT_o"]) for r in res.results]
    hT = [np.asarray(r["hT_o"]) for r in res.results]
    mT = run_B(0, hT)
    res = run_bass_kernel_spmd(_get("CA"), [dict(
        xT=xT[c], mT=mT[c], wo=wo_l[0], gains=_gain_layout([A(ffn2_norm)[0], A(ffn1_norm)[1], A(mix_norm)[1]]),
        wgB=ffn2[0][0], wuB=ffn2[0][1], wdB=ffn2[0][2],
        wgA=ffn1[1][0], wuA=ffn1[1][1], wdA=ffn1[1][2]) for c in cores], core_ids=cores)
    xT = [np.asarray(r["xT_o"]) for r in res.results]
    hT = [np.asarray(r["hT_o"]) for r in res.results]
    mT = run_B(1, hT)
    res = run_bass_kernel_spmd(_get("C1"), [dict(
        xT=xT[c], mT=mT[c], wo=wo_l[1], gains=_gain_layout([A(ffn2_norm)[1], A(final_norm)]),
        wgB=ffn2[1][0], wuB=ffn2[1][1], wdB=ffn2[1][2]) for c in cores], core_ids=cores)
    out = np.empty((4, 4096, D), f32)
    for c in cores:
        out[c // 2, (c % 2) * T:(c % 2 + 1) * T, :] = np.asarray(res.results[c]["xT_o"]).T
    return out
```

```python
import ml_dtypes
from concourse.bass_utils import run_bass_kernel_spmd
import contextlib
import concourse.bass as bass
import concourse.mybir as mybir

ENGS = ("pe", "act", "dve", "pool", "sp")
N_DMA_SEMS = {"sp": 12, "pool": 12, "act": 6}


class SemPool:
    def __init__(self, nc, stack):
        self.esem = {e: stack.enter_context(nc.semaphore("s_" + e)) for e in ENGS}
        self.dsem = {e: [stack.enter_context(nc.semaphore("d_%s%d" % (e, k))) for k in range(N_DMA_SEMS[e])]
                     for e in N_DMA_SEMS}
        self.ecnt = {e: 0 for e in ENGS}
        self.dcnt = {e: [0] * N_DMA_SEMS[e] for e in N_DMA_SEMS}
        self.dnext = {e: 0 for e in N_DMA_SEMS}
        self.csem = stack.enter_context(nc.semaphore("s_cc"))
        self.ccnt = 0
        self.stage = 0


class Sched:
    def __init__(self):
        self.ops = []
        self.res = {}

    def add(self, eng, fn, reads=(), writes=(), dma=False, extra_deps=(), cc=False):
        i = len(self.ops)
        deps = set(extra_deps)
        for k in reads:
            r = self.res.get(k)
            if r is not None and r[0] is not None:
                deps.add(r[0])
        for k in writes:
            r = self.res.get(k)
            if r is not None:
                if r[1]:
                    deps.update(r[1])
                elif r[0] is not None:
                    deps.add(r[0])
        for k in reads:
            self.res.setdefault(k, [None, []])[1].append(i)
        for k in writes:
            self.res[k] = [i, []]
        deps.discard(i)
        best = {}
        keep = set()
        for d in deps:
            od = self.ops[d]
            if od["dma"]:
                keep.add(d)
            else:
                b = best.get(od["eng"])
                if b is None or d > b:
                    best[od["eng"]] = d
        keep.update(best.values())
        self.ops.append(dict(eng=eng, fn=fn, deps=keep, dma=(dma or cc), sig=(dma or cc), cc=cc))
        return i

    def emit(self, nc, sp):
        ops = self.ops
        base_e = dict(sp.ecnt)
        dma_hist = {e: {} for e in N_DMA_SEMS}
        prev_cc = None
        base_cc = sp.ccnt
        for i, op in enumerate(ops):
            if op["cc"]:
                sp.ccnt += 1
                op["val"] = sp.ccnt
                prev_cc = i
                continue
            if op["dma"]:
                e = op["eng"]
                slot = sp.dnext[e]
                sp.dnext[e] = (slot + 1) % N_DMA_SEMS[e]
                op["slot"] = slot
                sp.dcnt[e][slot] += 16
                op["val"] = sp.dcnt[e][slot]
                if slot in dma_hist[e]:
                    op["deps"].add(dma_hist[e][slot])
                dma_hist[e][slot] = i
        for i, op in enumerate(ops):
            keep = set()
            for d in op["deps"]:
                od = ops[d]
                if (not od["dma"]) and od["eng"] == "pe" and op["eng"] == "pe" and not op["dma"]:
                    continue
                keep.add(d)
                od["sig"] = True
            op["deps"] = keep
        for e in ENGS:
            for op in reversed(ops):
                if op["eng"] == e and not op["dma"] and op["fn"] is not None:
                    op["sig"] = True
                    break
        for op in ops:
            if op["sig"] and not op["dma"]:
                sp.ecnt[op["eng"]] += 1
                op["val"] = sp.ecnt[op["eng"]]
        prev_e = base_e
        prev_d = {e: [sp.dcnt[e][k] for k in range(N_DMA_SEMS[e])] for e in N_DMA_SEMS}
        for op in ops:
            if op["dma"] and not op["cc"]:
                prev_d[op["eng"]][op["slot"]] -= 16

        def event(d):
            od = ops[d]
            if od["cc"]:
                return sp.csem, od["val"], ("c",)
            if od["dma"]:
                return sp.dsem[od["eng"]][od["slot"]], od["val"], ("d", od["eng"], od["slot"])
            return sp.esem[od["eng"]], od["val"], ("e", od["eng"])

        first_stage = sp.stage == 0
        sp.stage += 1
        with nc.Block() as block:
            def run(eng_name, e):
                waited = {}
                if not first_stage:
                    for e2 in ENGS:
                        if prev_e[e2] > 0 and e2 != eng_name:
                            e.wait_ge(sp.esem[e2], prev_e[e2])
                        waited[("e", e2)] = prev_e[e2]
                    for e2 in N_DMA_SEMS:
                        for k in range(N_DMA_SEMS[e2]):
                            if prev_d[e2][k] > 0:
                                e.wait_ge(sp.dsem[e2][k], prev_d[e2][k])
                            waited[("d", e2, k)] = prev_d[e2][k]
                    if base_cc > 0:
                        e.wait_ge(sp.csem, base_cc)
                    waited[("c",)] = base_cc
                for op in ops:
                    if op["eng"] != eng_name:
                        continue
                    need = {}
                    for d in op["deps"]:
                        s, v, key = event(d)
                        if waited.get(key, 0) >= v:
                            continue
                        if key not in need or need[key][1] < v:
                            need[key] = (s, v)
                    for key, (s, v) in need.items():
                        e.wait_ge(s, v)
                        waited[key] = v
                    if op["fn"] is None:
                        continue
                    ins = op["fn"](e)
                    if op["cc"]:
                        ins.then_inc(sp.csem)
                    elif op["dma"]:
                        ins.then_inc(sp.dsem[eng_name][op["slot"]], 16)
                    elif op["sig"]:
                        ins.then_inc(sp.esem[eng_name], 1)

            @block.tensor
            def _(e):
                run("pe", e)

            @block.scalar
            def _(e):
                run("act", e)

            @block.vector
            def _(e):
                run("dve", e)

            @block.gpsimd
            def _(e):
                run("pool", e)

            @block.sync
            def _(e):
                run("sp", e)


import numpy as np
import concourse.bass as bass
import concourse.mybir as mybir

F32 = mybir.dt.float32
BF16 = mybir.dt.bfloat16
AF = mybir.ActivationFunctionType
ALU = mybir.AluOpType

D = 1024
KC = 8
DFF = 2816
NJ = 22
T = 2048
TT = 512
NT = T // TT
EPS = 1e-6


def ts(i, n):
    return slice(i * n, (i + 1) * n)


class Ctx:
    pass


_UID = [0]


def _nm(name):
    return "%s_u%d" % (name, _UID[0])


def emit_norm(S, c, gkey, G32, on_tile=None, out=None):
    def A(tt):
        bank = 4 + tt
        for kc in range(KC):
            sb = (tt * KC + kc) % 2
            S.add("act", lambda e, kc=kc, tt=tt, sb=sb: e.activation(
                c.SQ[:, sb, :], c.XT[:, kc, ts(tt, TT)], AF.Square),
                reads=[("XT", kc, tt)], writes=[("SQ", sb)])
            S.add("pe", lambda e, kc=kc, sb=sb, bank=bank: e.matmul(
                c.PS[:, bank, :], c.ONES[:, :], c.SQ[:, sb, :], start=(kc == 0), stop=(kc == KC - 1)),
                reads=[("SQ", sb), ("ONES",)], writes=[("PS", bank)])

    def B(tt):
        bank = 4 + tt
        rb = tt % 2
        S.add("dve", lambda e, bank=bank, rb=rb: e.tensor_scalar(
            c.RSTD[:, rb, :], c.PS[:, bank, :], 1.0 / D, EPS, ALU.mult, ALU.add),
            reads=[("PS", bank)], writes=[("RSTD", rb)])
        S.add("dve", lambda e, rb=rb: e.reciprocal(c.RSTD[:, rb, :], c.RSTD[:, rb, :]),
              reads=[("RSTD", rb)], writes=[("RSTD", rb)])
        S.add("act", lambda e, rb=rb: e.activation(c.RSTD[:, rb, :], c.RSTD[:, rb, :], AF.Sqrt),
              reads=[("RSTD", rb)], writes=[("RSTD", rb)])

    def C(tt):
        rb = tt % 2
        for kc in range(KC):
            if out == "XT":
                S.add("dve", lambda e, kc=kc, tt=tt, rb=rb: e.scalar_tensor_tensor(
                    c.XT[:, kc, ts(tt, TT)], c.XT[:, kc, ts(tt, TT)], G32[:, kc:kc + 1], c.RSTD[:, rb, :],
                    ALU.mult, ALU.mult),
                    reads=[("XT", kc, tt), ("RSTD", rb), gkey], writes=[("XT", kc, tt)])
            else:
                S.add("dve", lambda e, kc=kc, tt=tt, rb=rb: e.scalar_tensor_tensor(
                    c.HT[:, kc, ts(tt, TT)], c.XT[:, kc, ts(tt, TT)], G32[:, kc:kc + 1], c.RSTD[:, rb, :],
                    ALU.mult, ALU.mult),
                    reads=[("XT", kc, tt), ("RSTD", rb), gkey], writes=[("HT", kc, tt)])
        if on_tile is not None:
            on_tile(tt)

    A(0)
    A(1)
    B(0)
    C(0)
    A(2)
    B(1)
    C(1)
    A(3)
    B(2)
    C(2)
    B(3)
    C(3)


def emit_ffn(S, c, wg, wu, wd, norm=None):
    def load_w(j):
        wb = j % 2
        S.add("pool", lambda e, j=j, wb=wb: e.dma_start(out=c.WG[wb], in_=wg[j]),
              writes=[("WG", wb)], dma=True)
        S.add("pool", lambda e, j=j, wb=wb: e.dma_start(out=c.WU[wb], in_=wu[j]),
              writes=[("WU", wb)], dma=True)

    cnt = [0]

    def up(j, tt):
        wb = j % 2
        pb = cnt[0] % 2
        cnt[0] += 1
        bg, bu = pb, 2 + pb
        for kc in range(KC):
            S.add("pe", lambda e, kc=kc, tt=tt, wb=wb, bg=bg: e.matmul(
                c.PS[:, bg, :], c.WG[wb][:, kc, :], c.HT[:, kc, ts(tt, TT)],
                start=(kc == 0), stop=(kc == KC - 1)),
                reads=[("WG", wb), ("HT", kc, tt)], writes=[("PS", bg)])
        for kc in range(KC):
            S.add("pe", lambda e, kc=kc, tt=tt, wb=wb, bu=bu: e.matmul(
                c.PS[:, bu, :], c.WU[wb][:, kc, :], c.HT[:, kc, ts(tt, TT)],
                start=(kc == 0), stop=(kc == KC - 1)),
                reads=[("WU", wb), ("HT", kc, tt)], writes=[("PS", bu)])
        S.add("act", lambda e, bg=bg, pb=pb: e.activation(c.SIL[:, pb, :], c.PS[:, bg, :], AF.Silu),
              reads=[("PS", bg)], writes=[("SIL", pb)])
        S.add("dve", lambda e, j=j, tt=tt, bu=bu, pb=pb: e.tensor_tensor(
            c.ACTT[:, j, ts(tt, TT)], c.SIL[:, pb, :], c.PS[:, bu, :], ALU.mult),
            reads=[("SIL", pb), ("PS", bu)], writes=[("ACTT", j, tt)])

    load_w(0)
    load_w(1)
    if norm is not None:
        emit_norm(S, c, norm[0], norm[1], on_tile=lambda tt: (up(0, tt), up(1, tt)))
    else:
        for tt in range(NT):
            up(0, tt)
            up(1, tt)
    for j in range(2, NJ):
        load_w(j)
        for tt in range(NT):
            up(j, tt)
    ALIAS = [("WG", 0), ("WG", 1), ("WU", 0)]
    for cc in range(KC):
        wb = cc % 2
        wkeys = [("WD", 0)] if wb == 0 else ALIAS
        S.add("pool", lambda e, cc=cc, wb=wb: e.dma_start(out=c.WD[wb], in_=wd[cc]),
              writes=wkeys, dma=True)
        for j in range(NJ):
            for tt in range(NT):
                bank = 4 * wb + tt
                S.add("pe", lambda e, j=j, tt=tt, wb=wb, bank=bank: e.matmul(
                    c.PS[:, bank, :], c.WD[wb][:, j, :], c.ACTT[:, j, ts(tt, TT)],
                    start=(j == 0), stop=(j == NJ - 1)),
                    reads=wkeys + [("ACTT", j, tt)], writes=[("PS", bank)])
        for tt in range(NT):
            bank = 4 * wb + tt
            S.add("dve", lambda e, cc=cc, tt=tt, bank=bank: e.scalar_tensor_tensor(
                c.XT[:, cc, ts(tt, TT)], c.PS[:, bank, :], 0.5, c.XT[:, cc, ts(tt, TT)],
                ALU.mult, ALU.add),
                reads=[("PS", bank), ("XT", cc, tt)], writes=[("XT", cc, tt)])


import math, contextlib, os
NSTAGE = int(os.environ.get('NSTAGE', '3'))
D3 = int(os.environ.get('D3', '9'))
import numpy as np
import concourse.bass as bass
import concourse.mybir as mybir

S_LEN = 4096
NB = 32
NQC = 8
WCOLS = 1536
STRIP = 2560
DSTRIP = 3072
DILS = (1, 4, 16)
NEG = -30000.0


def t5_bucket_np(dist):
    n = np.maximum(dist, 0)
    nf = np.maximum(n, 1).astype(np.float32)
    large = 16 + (np.log(nf / np.float32(16)) / np.float32(math.log(2048 / 16)) * np.float32(16)).astype(np.int32)
    large = np.minimum(large, 31)
    return np.where(n < 16, n, large)


def prep_B_consts(rel_bias, g):
    p = np.arange(128)[:, None]
    c = np.arange(STRIP)[None, :]
    dd = c - p - 384
    bk = t5_bucket_np(dd)
    strips = np.empty((128, 4, STRIP), np.float32)
    for hh in range(2):
        for m in range(2):
            col = (2 * g + hh) * 2 + m
            strips[:, hh * 2 + m, :] = np.where(dd >= 0, rel_bias[bk, col], np.float32(NEG))
    c = np.arange(DSTRIP)[None, :]
    dd = c - p - 384
    bk = t5_bucket_np(dd)
    dtile = np.empty((128, 5, DSTRIP), np.float32)
    for h in range(4):
        dtile[:, h, :] = rel_bias[bk, 8 + 4 * g + h]
    ddc = np.maximum(dd, 0)
    cnt = ((ddc <= 128).astype(np.float32) + ((ddc <= 512) & (ddc % 4 == 0)).astype(np.float32)
           + ((ddc <= 2048) & (ddc % 16 == 0)).astype(np.float32))
    dtile[:, 4, :] = np.where(dd >= 0, cnt, 0.0)
    return strips, dtile


def prep_B_weights(w_in_l, g):
    cols = []
    for base in (0, 512, 1024, 1536, 2048, 2560):
        cols.append(w_in_l[:, base + g * 256: base + (g + 1) * 256])
    w = np.concatenate(cols, axis=1)
    return np.ascontiguousarray(w.reshape(8, 128, WCOLS).transpose(1, 0, 2))


def build_B(layer):
    lam0 = 0.8 - 0.6 * math.exp(-0.3 * layer)
    nc = bass.Bass("TRN2", target_bir_lowering=False)
    hT = nc.dram_tensor("hT", [2, 1024, 2048], BF16, kind="ExternalInput").ap()
    w = nc.dram_tensor("w", [128, 8, WCOLS], F32, kind="ExternalInput").ap()
    strips = nc.dram_tensor("strips", [128, 4, STRIP], F32, kind="ExternalInput").ap()
    dtile = nc.dram_tensor("dtile", [128, 5, DSTRIP], F32, kind="ExternalInput").ap()
    lamv = nc.dram_tensor("lamv", [128, 4, 64], F32, kind="ExternalInput").ap()
    sgain = nc.dram_tensor("sgain", [128, 128], F32, kind="ExternalInput").ap()
    ident = nc.dram_tensor("ident", [128, 128], F32, kind="ExternalInput").ap()
    mixT = nc.dram_tensor("mixT", [512, S_LEN], BF16, kind="ExternalOutput").ap()
    emit_B(nc, lam0, hT, w, strips, dtile, lamv, sgain, ident, mixT)
    return nc


def emit_B(nc, lam0, hT, w, strips, dtile, lamv, sgain, ident, mixT, sp=None, final_wait=True, PS=None,
           hT_src=None, tail=None, mix_dst=None, tail2=None, head=None):
    _UID[0] += 1
    with contextlib.ExitStack() as st:
        def sb(name, shape, dt):
            return st.enter_context(nc.sbuf_tensor(_nm(name), shape, dt))
        if sp is None:
            sp = SemPool(nc, st)
        if PS is None:
            PS = st.enter_context(nc.psum_tensor("PSB", [128, 8, 512], F32))
        if hT_src is None:
            hT_src = lambda hf, kc, ch: hT[hf, ts(kc, 128), ch * 1024:(ch + 1) * 1024]
        if mix_dst is None:
            mix_dst = lambda kind, i: (mixT[ts(i, 128), :] if kind == "a" else mixT[256 + i * 128: 256 + (i + 1) * 128, :])
        QBT = sb("QBT", [128, 2, S_LEN], BF16)
        KBT = sb("KBT", [128, 2, S_LEN], BF16)
        VB = sb("VB", [128, 1, NB, 4, 65], BF16)
        IDN = sb("IDN", [128, 128], BF16)
        ONE32 = sb("ONE32", [128, 64], F32)
        LAM = sb("LAM", [128, 8], F32)
        SG = sb("SG", [128, 128], F32)
        sd = contextlib.ExitStack()
        QAT = sd.enter_context(nc.sbuf_tensor(_nm("QAT"), [128, 2, S_LEN], BF16))
        KAT = sd.enter_context(nc.sbuf_tensor(_nm("KAT"), [128, 2, S_LEN], BF16))
        VA = sd.enter_context(nc.sbuf_tensor(_nm("VA"), [128, NB, 2, 129], BF16))
        with contextlib.ExitStack() as s1:
            def sb1(name, shape, dt):
                return s1.enter_context(nc.sbuf_tensor(_nm(name), shape, dt))
            S = Sched()
            HTs = sb1("HTs", [128, 2, 8, 2048], BF16)
            W = sb1("W", [128, 8, WCOLS], BF16)
            LV = sb1("LV", [128, 4, 64], F32)
            hpc = [0]
            if head is not None:
                head(S)
            S.add("pool", lambda e: e.dma_start(out=W[:, :, :], in_=w), writes=[("W",)], dma=True)
            S.add("pool", lambda e: e.dma_start(out=IDN[:, :], in_=ident), writes=[("IDN",)], dma=True)
            S.add("sp", lambda e: e.dma_start(out=LV[:, :, :], in_=lamv), writes=[("LV",)], dma=True)
            S.add("sp", lambda e: e.dma_start(out=SG[:, :], in_=sgain), writes=[("SG",)], dma=True)
            S.add("dve", lambda e: e.memset(ONE32[:, :], 1.0), writes=[("ONE32",)])
            S.add("pool", lambda e: e.memset(VA[:, :, :, 128:129], 1.0), writes=[("VAones",)])
            S.add("pool", lambda e: e.memset(VB[:, :, :, :, 64:65], 1.0), writes=[("VBones",)])
            S.add("dve", lambda e: e.tensor_tensor(LV[:, 0, :], LV[:, 0, :], LV[:, 1, :], ALU.mult),
                  reads=[("LV",)], writes=[("LV",)])
            S.add("dve", lambda e: e.tensor_tensor(LV[:, 2, :], LV[:, 2, :], LV[:, 3, :], ALU.mult),
                  reads=[("LV",)], writes=[("LV",)])
            S.add("dve", lambda e: e.tensor_reduce(LAM[:, 2:3], LV[:, 0, :], mybir.AxisListType.X, ALU.add),
                  reads=[("LV",)], writes=[("LAM",)])
            S.add("dve", lambda e: e.tensor_reduce(LAM[:, 3:4], LV[:, 2, :], mybir.AxisListType.X, ALU.add),
                  reads=[("LV",)], writes=[("LAM",)])
            S.add("act", lambda e: e.activation(LAM[:, 4:6], LAM[:, 2:4], AF.Exp), reads=[("LAM",)], writes=[("LAM",)])
            S.add("dve", lambda e: e.tensor_tensor(LAM[:, 0:1], LAM[:, 4:5], LAM[:, 5:6], ALU.subtract),
                  reads=[("LAM",)], writes=[("LAM",)])
            S.add("dve", lambda e: e.tensor_scalar(LAM[:, 0:1], LAM[:, 0:1], float(lam0), None, ALU.add),
                  reads=[("LAM",)], writes=[("LAM",)])
            S.add("dve", lambda e: e.tensor_scalar(LAM[:, 1:2], LAM[:, 0:1], -1.0, None, ALU.mult),
                  reads=[("LAM",)], writes=[("LAM",)])
            S.add("dve", lambda e: e.tensor_scalar(SG[:, :], SG[:, :], float(1.0 - lam0), None, ALU.mult),
                  reads=[("SG",)], writes=[("SG",)])
            cp = [0]

            def evac(out_ap, in_ap, reads, writes):
                eng = "act" if cp[0] % 2 == 0 else "dve"
                cp[0] += 1
                if eng == "act":
                    S.add("act", lambda e: e.copy(out_ap, in_ap), reads=reads, writes=writes)
                else:
                    S.add("dve", lambda e: e.tensor_copy(out_ap, in_ap), reads=reads, writes=writes)

            pbank = [0]

            def nextbank():
                b = pbank[0]
                pbank[0] = (b + 1) % 8
                return b

            for ch in range(2):
                for hf in range(2):
                    for kc in range(8):
                        S.add("sp" if kc % 2 == 0 else "act", lambda e, kc=kc, hf=hf, ch=ch: e.dma_start(
                            out=HTs[:, hf, kc, ch * 1024:(ch + 1) * 1024], in_=hT_src(hf, kc, ch)),
                            reads=[("HA", ch)], writes=[("HTs", hf, kc, ch)], dma=True)
            for ch, hf, tt in [(ch, hf, ch * 2 + t2) for ch in range(2) for hf in range(2) for t2 in range(2)]:
                if True:
                    qc = hf * 4 + tt
                    for (dst, di, c0) in ((QAT, 0, 0), (QAT, 1, 128), (KAT, 0, 256), (KAT, 1, 384),
                                          (QBT, 0, 768), (QBT, 1, 896), (KBT, 0, 1024), (KBT, 1, 1152)):
                        b = nextbank()
                        for kc in range(8):
                            S.add("pe", lambda e, kc=kc, tt=tt, c0=c0, b=b, hf=hf: e.matmul(
                                PS[:, b, :], W[:, kc, c0:c0 + 128], HTs[:, hf, kc, ts(tt, 512)],
                                start=(kc == 0), stop=(kc == 7)),
                                reads=[("W",), ("HTs", hf, kc, ch)], writes=[("PS", b)])
                        evac(dst[:, di, ts(qc, 512)], PS[:, b, :], [("PS", b)], [("QK", id(dst), di, qc)])
                    for bl in range(4):
                        blk = qc * 4 + bl
                        b = nextbank()
                        for kc in range(8):
                            S.add("pe", lambda e, kc=kc, tt=tt, bl=bl, b=b, hf=hf: e.matmul(
                                PS[:, b, 0:256], HTs[:, hf, kc, tt * 512 + bl * 128: tt * 512 + (bl + 1) * 128],
                                W[:, kc, 512:768], start=(kc == 0), stop=(kc == 7)),
                                reads=[("W",), ("HTs", hf, kc, ch)], writes=[("PS", b)])
                        evac(VA[:, blk, :, 0:128], PS[:, b, 0:256].rearrange("p (h d) -> p h d", h=2),
                             [("PS", b), ("VAones",)], [("VA", blk)])
                        b = nextbank()
                        for kc in range(8):
                            S.add("pe", lambda e, kc=kc, tt=tt, bl=bl, b=b, hf=hf: e.matmul(
                                PS[:, b, 0:256], HTs[:, hf, kc, tt * 512 + bl * 128: tt * 512 + (bl + 1) * 128],
                                W[:, kc, 1280:1536], start=(kc == 0), stop=(kc == 7)),
                                reads=[("W",), ("HTs", hf, kc, ch)], writes=[("PS", b)])
                        evac(VB[:, 0, blk, :, 0:64], PS[:, b, 0:256].rearrange("p (h d) -> p h d", h=4),
                             [("PS", b), ("VBones",)], [("VB", 0, blk)])
            S.emit(nc, sp)
        def attn(S, sbx, kind, Qt, Kt, vfn, B8, npair, outs, mixname):
            W = 129 if kind == "a" else 65
            PT = sbx("PT" + kind, [128, int(os.environ.get("AVLAG", "2")) + 2, 2, 512], BF16)
            OS = sbx("OS" + kind, [128, 2, 8, W], F32)
            OA = sbx("OA" + kind, [128, 4, 128], F32)
            SQ = sbx("SQ" + kind, [128, 4, 128], F32)
            OAN = sbx("OAN" + kind, [128, 2, 4, 128], BF16)
            RC = sbx("RC" + kind, [128, 2, 12], F32)
            MIX = sbx(mixname, [128, 2, S_LEN], BF16)

            def oloc(i, sbi):
                if kind == "a":
                    idx = i * 4 + sbi
                    return 4 + idx // 3, (idx % 3) * W, (idx % 3 == 0)
                return 4 + i, sbi * W, (sbi == 0)
            tiles = []
            for pair in range(npair):
                for qc in range(NQC):
                    kb0 = 0 if kind == "a" else max(0, 4 * qc - 16)
                    for kb in range(kb0, 4 * qc + 4):
                        tiles.append((pair, qc, kb, kb0))
            AVLAG = int(os.environ.get('AVLAG', '2'))
            NPT = AVLAG + 2

            def cols(t):
                pair, qc, kb, kb0 = tiles[t]
                s_lo = max(0, kb - 4 * qc)
                s_hi = 3 if kind == "a" else min(3, 16 + kb - 4 * qc)
                return s_lo * 128, (s_hi + 1) * 128

            def emit_qk(t):
                pair, qc, kb, kb0 = tiles[t]
                sb0 = 2 * (t % 2)
                lo, hi = cols(t)
                for i in range(2):
                    S.add("pe", lambda e, i=i, pair=pair, kb=kb, qc=qc, sb0=sb0, lo=lo, hi=hi: e.matmul(
                        PS[:, sb0 + i, lo:hi], Kt[i * 64:(i + 1) * 64, pair, ts(kb, 128)],
                        Qt[i * 64:(i + 1) * 64, pair, qc * 512 + lo:qc * 512 + hi], start=True, stop=True),
                        writes=[("PS", sb0 + i)])

            def emit_act(t):
                pair, qc, kb, kb0 = tiles[t]
                sb0 = 2 * (t % 2)
                pt = t % NPT
                lo, hi = cols(t)
                delta = 512 * qc - 128 * kb
                c0 = min(delta + 384, 2048) if kind == "a" else delta + 384
                S.add("act", lambda e, sb0=sb0, pt=pt, lo=lo, hi=hi: e.activation(
                    PT[:, pt, :, lo:hi], PS[:, sb0:sb0 + 2, lo:hi], AF.Exp, scale=0.125),
                    reads=[("PS", sb0), ("PS", sb0 + 1)], writes=[("PT", pt)])
                S.add("dve", lambda e, pt=pt, pair=pair, c0=c0, lo=lo, hi=hi: e.tensor_tensor(
                    PT[:, pt, :, lo:hi], PT[:, pt, :, lo:hi], B8[:, pair * 2:pair * 2 + 2, c0 + lo:c0 + hi], ALU.mult),
                    reads=[("PT", pt), ("B8", pair * 2), ("B8", pair * 2 + 1)], writes=[("PT", pt)])

            def av_list(pair, qc, kb, kb0):
                out = []
                for i in range(2):
                    for sbi in range(4):
                        qi = 4 * qc + sbi
                        if kb > qi or (kind == "b" and qi - kb > 16):
                            continue
                        out.append((i, sbi))
                return out

            last_in_bank = {}
            for (pair, qc, kb, kb0) in tiles:
                for (i, sbi) in av_list(pair, qc, kb, kb0):
                    last_in_bank[(pair, qc, oloc(i, sbi)[0])] = (kb, i, sbi)

            def emit_av(t):
                pair, qc, kb, kb0 = tiles[t]
                pt = t % NPT
                for (i, sbi) in av_list(pair, qc, kb, kb0):
                    ob, oc, first = oloc(i, sbi)
                    S.add("pe", lambda e, pt=pt, i=i, sbi=sbi, kb=kb, pair=pair, ob=ob, oc=oc,
                          st_=(kb == kb0 and first), sp_=(last_in_bank[(pair, qc, ob)] == (kb, i, sbi)): e.matmul(
                        PS[:, ob, oc:oc + W], PT[:, pt, i, ts(sbi, 128)], vfn(kb, pair, i),
                        start=st_, stop=sp_),
                        reads=[("PT", pt)], writes=[("PSO", ob)])

            fin = 0
            pending = []
            KPOP = int(os.environ.get('KPOP', '1'))

            def DF(*a_, **k_):
                pending.append(lambda: S.add(*a_, **k_))

            def DFO(*a_, **k_):
                pending.append(lambda: outs.append(S.add(*a_, **k_)))
            ntl = len(tiles)
            emit_qk(0)
            for t in range(ntl + AVLAG):
                if t + 1 < ntl:
                    emit_qk(t + 1)
                if t < ntl:
                    emit_act(t)
                for _ in range(KPOP):
                    if pending:
                        pending.pop(0)()
                ta = t - AVLAG
                if ta < 0:
                    continue
                emit_av(ta)
                pair, qc, kb, kb0 = tiles[ta]
                if kb == 4 * qc + 3:
                    while pending:
                        pending.pop(0)()
                    f = fin % 2
                    fin += 1
                    if kind == "a":
                        for bnk, n in ((4, 3), (5, 3), (6, 2)):
                            eng = "act" if bnk == 5 else "dve"
                            dst = OS[:, f, (bnk - 4) * 3:(bnk - 4) * 3 + n, :]
                            src = PS[:, bnk, 0:n * W].rearrange("p (a w) -> p a w", w=W)
                            if eng == "act":
                                S.add("act", lambda e, dst=dst, src=src: e.copy(dst, src),
                                      reads=[("PSO", bnk)], writes=[("OS", f, bnk), ("PSO", bnk)])
                            else:
                                S.add("dve", lambda e, dst=dst, src=src: e.tensor_copy(dst, src),
                                      reads=[("PSO", bnk)], writes=[("OS", f, bnk), ("PSO", bnk)])
                        oskeys = [("OS", f, 4), ("OS", f, 5), ("OS", f, 6)]
                    else:
                        for i in range(2):
                            dst = OS[:, f, i * 4:(i + 1) * 4, :]
                            src = PS[:, 4 + i, 0:4 * W].rearrange("p (a w) -> p a w", w=W)
                            if i == 0:
                                S.add("act", lambda e, dst=dst, src=src: e.copy(dst, src),
                                      reads=[("PSO", 4 + i)], writes=[("OS", f, 4 + i), ("PSO", 4 + i)])
                            else:
                                S.add("dve", lambda e, dst=dst, src=src: e.tensor_copy(dst, src),
                                      reads=[("PSO", 4 + i)], writes=[("OS", f, 4 + i), ("PSO", 4 + i)])
                        oskeys = [("OS", f, 4), ("OS", f, 5)]
                    DF("dve", lambda e, f=f: e.reciprocal(RC[:, f, 0:8], OS[:, f, :, W - 1]),
                          reads=oskeys, writes=[("RC", f)])
                    if kind == "a":
                        DF("dve", lambda e, f=f: e.tensor_scalar(RC[:, f, 4:8], RC[:, f, 4:8], LAM[:, 1:2], None, ALU.mult),
                              reads=[("RC", f)], writes=[("RC", f)])
                        for sbi in range(4):
                            DF("dve", lambda e, f=f, sbi=sbi: e.tensor_scalar(
                                OA[:, sbi, :], OS[:, f, sbi, 0:128], RC[:, f, sbi:sbi + 1], None, ALU.mult),
                                reads=oskeys + [("RC", f)], writes=[("OA", sbi)])
                            DF("dve", lambda e, f=f, sbi=sbi: e.scalar_tensor_tensor(
                                OA[:, sbi, :], OS[:, f, 4 + sbi, 0:128], RC[:, f, 4 + sbi:5 + sbi], OA[:, sbi, :],
                                ALU.mult, ALU.add),
                                reads=oskeys + [("RC", f), ("OA", sbi)], writes=[("OA", sbi)])
                        DF("pool", lambda e: e.tensor_tensor(SQ[:, :, :], OA[:, :, :], OA[:, :, :], ALU.mult),
                              reads=[("OA", k) for k in range(4)], writes=[("SQ",)])
                        DF("dve", lambda e, f=f: e.tensor_reduce(RC[:, f, 8:12], SQ[:, :, :], mybir.AxisListType.X, ALU.add),
                              reads=[("SQ",)], writes=[("RC", f)])
                        DF("dve", lambda e, f=f: e.tensor_scalar(
                            RC[:, f, 8:12], RC[:, f, 8:12], 1.0 / 128, EPS, ALU.mult, ALU.add),
                            reads=[("RC", f)], writes=[("RC", f)])
                        DF("dve", lambda e, f=f: e.reciprocal(RC[:, f, 8:12], RC[:, f, 8:12]),
                              reads=[("RC", f)], writes=[("RC", f)])
                        DF("act", lambda e, f=f: e.activation(RC[:, f, 8:12], RC[:, f, 8:12], AF.Sqrt),
                              reads=[("RC", f)], writes=[("RC", f)])
                        for sbi in range(4):
                            DF("dve", lambda e, f=f, sbi=sbi: e.scalar_tensor_tensor(
                                OAN[:, f, sbi, :], OA[:, sbi, :], RC[:, f, 8 + sbi:9 + sbi], SG[:, :], ALU.mult, ALU.mult),
                                reads=[("OA", sbi), ("RC", f)], writes=[("OAN", f, sbi)])
                    else:
                        for sbi in range(4):
                            for i in range(2):
                                DF("dve", lambda e, f=f, sbi=sbi, i=i: e.tensor_scalar(
                                    OAN[:, f, sbi, i * 64:(i + 1) * 64], OS[:, f, i * 4 + sbi, 0:64],
                                    RC[:, f, i * 4 + sbi:i * 4 + sbi + 1], None, ALU.mult),
                                    reads=oskeys + [("RC", f)], writes=[("OAN", f, sbi)])
                    if True:
                        for sbi in range(4):
                            DF("pe", lambda e, f=f, sbi=sbi: e.transpose(
                                PS[:, 7, :].bitcast(BF16)[:, sbi * 128:(sbi + 1) * 128], OAN[:, f, sbi, :], IDN[:, :]),
                                reads=[("OAN", f, sbi)], writes=[("PS", 7)])
                        DF("act", lambda e, pair=pair, qc=qc: e.copy(
                            MIX[:, pair, ts(qc, 512)], PS[:, 7, :].bitcast(BF16)[:, 0:512]),
                            reads=[("PS", 7)], writes=[("MIX", pair, qc)])
                        if qc == NQC - 1:
                            DFO("sp", lambda e, pair=pair: e.dma_start(
                                out=mix_dst(kind, pair), in_=MIX[:, pair, :]),
                                reads=[("MIX", pair, q) for q in range(NQC)], dma=True)
            while pending:
                pending.pop(0)()

        outs = []
        if NSTAGE < 2:
            sd.close()
            return
        with contextlib.ExitStack() as s2:
            def sb2(name, shape, dt):
                return s2.enter_context(nc.sbuf_tensor(_nm(name), shape, dt))
            S = Sched()
            B8 = sb2("B8A", [128, 4, STRIP], BF16)
            STG = sb2("STG2", [128, 2, STRIP], F32)
            for m in range(4):
                S.add("sp" if m % 2 == 0 else "act", lambda e, m=m: e.dma_start(out=STG[:, m % 2, :], in_=strips[:, m, :]),
                      writes=[("STG", m % 2)], dma=True)
                S.add("act", lambda e, m=m: e.activation(B8[:, m, :], STG[:, m % 2, :], AF.Exp),
                      reads=[("STG", m % 2)], writes=[("B8", m)])
            attn(S, sb2, "a", QAT, KAT, lambda kb, pair, i: VA[:, kb, pair, :], B8, 2, outs, "MIXA")
            if tail2 is not None:
                tail2(S, outs)
            S.add("sp", None, extra_deps=outs)
            S.emit(nc, sp)
        sd.close()
        if NSTAGE < 3:
            return
        with contextlib.ExitStack() as s3:
            def sb3(name, shape, dt):
                return s3.enter_context(nc.sbuf_tensor(_nm(name), shape, dt))
            S = Sched()
            B8 = sb3("B8B", [128, 4, DSTRIP], BF16)
            STG = sb3("STG3", [128, 2, DSTRIP], F32)
            CNT = sb3("CNT", [128, DSTRIP], F32)
            S.add("pool", lambda e: e.dma_start(out=CNT[:, :], in_=dtile[:, 4, :]), writes=[("CNT",)], dma=True)
            for h in range(4):
                S.add("sp" if h % 2 == 0 else "act", lambda e, h=h: e.dma_start(out=STG[:, h % 2, :], in_=dtile[:, h, :]),
                      writes=[("STG", h % 2)], dma=True)
                S.add("act", lambda e, h=h: e.activation(STG[:, h % 2, :], STG[:, h % 2, :], AF.Exp),
                      reads=[("STG", h % 2)], writes=[("STG", h % 2)])
                S.add("dve", lambda e, h=h: e.tensor_tensor(B8[:, h, :], STG[:, h % 2, :], CNT[:, :], ALU.mult),
                    reads=[("STG", h % 2), ("CNT",)], writes=[("B8", h)])
            outs = []
            attn(S, sb3, "b", QBT, KBT, lambda kb, pair, i: VB[:, 0, kb, 2 * pair + i, :], B8, 2, outs, "MIXB")
            if tail is not None:
                tail(S, outs)
            if final_wait:
                S.add("sp", None, extra_deps=outs)
            S.emit(nc, sp)


RG_PAIRS = [[0, 1], [2, 3], [4, 5], [6, 7]]
NG = 7


def emit_wout(S, c, wo):
    for cc in range(KC):
        wb = cc % 2
        S.add("pool", lambda e, cc=cc, wb=wb: e.dma_start(out=c.WG[wb], in_=wo[cc]), writes=[("WG", wb)], dma=True)
        for tt in range(NT):
            bank = 4 * wb + tt
            for kc in range(KC):
                S.add("pe", lambda e, kc=kc, tt=tt, wb=wb, bank=bank: e.matmul(
                    c.PS[:, bank, :], c.WG[wb][:, kc, :], c.HT[:, kc, ts(tt, TT)],
                    start=(kc == 0), stop=(kc == KC - 1)),
                    reads=[("WG", wb), ("HT", kc, tt)], writes=[("PS", bank)])
            S.add("dve", lambda e, cc=cc, tt=tt, bank=bank: e.tensor_tensor(
                c.XT[:, cc, ts(tt, TT)], c.XT[:, cc, ts(tt, TT)], c.PS[:, bank, :], ALU.add),
                reads=[("PS", bank), ("XT", cc, tt)], writes=[("XT", cc, tt)])


def emit_final_norm(S, c, gkey, G):
    for tt in range(NT):
        bank = 6 + (tt % 2)
        for kc in range(KC):
            sb = (tt * KC + kc) % 2
            S.add("act", lambda e, kc=kc, tt=tt, sb=sb: e.activation(
                c.SQ[:, sb, :], c.XT[:, kc, ts(tt, TT)], AF.Square),
                reads=[("XT", kc, tt)], writes=[("SQ", sb)])
            S.add("pe", lambda e, kc=kc, sb=sb, bank=bank: e.matmul(
                c.PS[:, bank, :], c.ONES[:, :], c.SQ[:, sb, :], start=(kc == 0), stop=(kc == KC - 1)),
                reads=[("SQ", sb)], writes=[("PS", bank)])
        rb = tt % 2
        S.add("dve", lambda e, bank=bank, rb=rb: e.tensor_scalar(
            c.RSTD[:, rb, :], c.PS[:, bank, :], 1.0 / D, EPS, ALU.mult, ALU.add),
            reads=[("PS", bank)], writes=[("RSTD", rb)])
        S.add("dve", lambda e, rb=rb: e.reciprocal(c.RSTD[:, rb, :], c.RSTD[:, rb, :]),
              reads=[("RSTD", rb)], writes=[("RSTD", rb)])
        S.add("act", lambda e, rb=rb: e.activation(c.RSTD[:, rb, :], c.RSTD[:, rb, :], AF.Sqrt),
              reads=[("RSTD", rb)], writes=[("RSTD", rb)])
        for kc in range(KC):
            S.add("dve", lambda e, kc=kc, tt=tt, rb=rb: e.scalar_tensor_tensor(
                c.XT[:, kc, ts(tt, TT)], c.XT[:, kc, ts(tt, TT)], G[:, kc:kc + 1], c.RSTD[:, rb, :],
                ALU.mult, ALU.mult),
                reads=[("XT", kc, tt), ("RSTD", rb), gkey], writes=[("XT", kc, tt)])


def emit_T(nc, sp, PS, x_src, x_dst, gains, sel, gidx, ffnA=None, ffnB=None, wo=None, m_src=None,
           h_dst=None, final=False, tail=None):
    _UID[0] += 1
    S = Sched()
    c = Ctx()
    c.PS = PS
    with contextlib.ExitStack() as st:
        def sb(name, shape, dt):
            return st.enter_context(nc.sbuf_tensor(_nm(name), shape, dt))
        c.XT = sb("XT", [128, KC, T], F32)
        c.HT = sb("HT", [128, KC, T], BF16)
        c.ACTT = sb("ACTT", [128, NJ, T], BF16)
        SCR = sb("SCR", [128, 4096], BF16)
        c.WG = [SCR[:, 0:1024].rearrange("p (k n) -> p k n", k=KC), SCR[:, 1024:2048].rearrange("p (k n) -> p k n", k=KC)]
        c.WU = [SCR[:, 2048:3072].rearrange("p (k n) -> p k n", k=KC), SCR[:, 3072:4096].rearrange("p (k n) -> p k n", k=KC)]
        WD0 = sb("WD0", [128, NJ, 128], BF16)
        c.WD = [WD0[:, :, :], SCR[:, 0:NJ * 128].rearrange("p (j n) -> p j n", j=NJ)]
        c.SQ = sb("SQ", [128, 2, TT], BF16)
        c.RSTD = sb("RSTD", [128, 2, TT], F32)
        c.SIL = sb("SIL", [128, 2, TT], BF16)
        c.ONES = sb("ONES", [128, 128], BF16)
        G = sb("G", [128, NG, KC], F32)
        SEL = sb("SEL", [128, 2], F32)
        S.add("dve", lambda e: e.memset(c.ONES[:, :], 1.0), writes=[("ONES",)])
        S.add("sp", lambda e: e.dma_start(out=G[:, :, :], in_=gains), writes=[("G",)], dma=True)
        S.add("sp", lambda e: e.dma_start(out=SEL[:, :], in_=sel), writes=[("SEL",)], dma=True)
        def load_x():
            for kc in range(KC):
                S.add("sp", lambda e, kc=kc: e.dma_start(out=c.XT[:, kc, :], in_=x_src(kc)),
                      writes=[("XT", kc, tt) for tt in range(NT)], dma=True)
        gi = iter(gidx)
        if wo is None:
            load_x()
        if wo is not None:
            for kc in range(KC):
                S.add("sp", lambda e, kc=kc: e.dma_start(out=c.HT[:, kc, :], in_=m_src(kc, 0)),
                      writes=[("HT", kc, tt) for tt in range(NT)], dma=True)
                S.add("act", lambda e, kc=kc: e.dma_start(out=c.ACTT[:, kc, :], in_=m_src(kc, 1)),
                      writes=[("ACTT", kc, tt) for tt in range(NT)], dma=True)
            load_x()
            for kc in range(KC):
                S.add("act", lambda e, kc=kc: e.activation(
                    c.HT[:, kc, :], c.HT[:, kc, :], AF.Copy, scale=SEL[:, 0:1]),
                    reads=[("SEL",)] + [("HT", kc, tt) for tt in range(NT)], writes=[("HT", kc, tt) for tt in range(NT)])
                S.add("dve", lambda e, kc=kc: e.scalar_tensor_tensor(
                    c.HT[:, kc, :], c.ACTT[:, kc, :], SEL[:, 1:2], c.HT[:, kc, :], ALU.mult, ALU.add),
                    reads=[("SEL",)] + [("HT", kc, tt) for tt in range(NT)] + [("ACTT", kc, tt) for tt in range(NT)],
                    writes=[("HT", kc, tt) for tt in range(NT)])
            emit_wout(S, c, wo)
            g = next(gi)
            emit_ffn(S, c, *ffnB, norm=(("G",), G[:, g, :]))
        outs = []
        houts = []
        if ffnA is not None:
            g = next(gi)
            emit_ffn(S, c, *ffnA, norm=(("G",), G[:, g, :]))
            g = next(gi)
            emit_norm(S, c, ("G",), G[:, g, :])
            for ch in range(2):
                for kc in range(KC):
                    houts.append(S.add("sp", lambda e, kc=kc, ch=ch: e.dma_start(
                        out=h_dst(kc, ch), in_=c.HT[:, kc, ch * (T // 2):(ch + 1) * (T // 2)]),
                        reads=[("HT", kc, 2 * ch), ("HT", kc, 2 * ch + 1)], dma=True))
                if ch == 0 and tail is not None:
                    tail(S, list(houts))
        if final:
            g = next(gi)
            emit_norm(S, c, ("G",), G[:, g, :], out="XT")
        for kc in range(KC):
            outs.append(S.add("sp", lambda e, kc=kc: e.dma_start(out=x_dst(kc), in_=c.XT[:, kc, :]),
                              reads=[("XT", kc, tt) for tt in range(NT)], dma=True))
        S.add("sp", None, extra_deps=outs + houts)
        S.emit(nc, sp)


def build_fused():
    nc = bass.Bass("TRN2", target_bir_lowering=False)

    def din(name, shape, dt=F32):
        return nc.dram_tensor(name, shape, dt, kind="ExternalInput").ap()
    xT = din("xT", [D, T])
    gains = din("gains", [128, NG, KC])
    sel = din("sel", [128, 2])
    ffn = [[(din("wg%d%d" % (l, k), [NJ, 128, KC, 128]), din("wu%d%d" % (l, k), [NJ, 128, KC, 128]),
             din("wd%d%d" % (l, k), [KC, 128, NJ, 128])) for k in range(2)] for l in range(2)]
    wo = [din("wo%d" % l, [KC, 128, KC, 128]) for l in range(2)]
    wB = [din("wB%d" % l, [128, 8, WCOLS]) for l in range(2)]
    lamv = [din("lamv%d" % l, [128, 4, 64]) for l in range(2)]
    sgain = [din("sgain%d" % l, [128, 128]) for l in range(2)]
    strips = din("strips", [128, 4, STRIP])
    dtile = din("dtile", [128, 5, DSTRIP])
    ident = din("ident", [128, 128])
    xT_o = nc.dram_tensor("xT_o", [D, T], F32, kind="ExternalOutput").ap()
    XSP = nc.dram_tensor("xsp", [D, T], F32)
    HS = [[nc.dram_tensor("hs%d%d" % (l, k), [D, T // 2], BF16) for k in range(2)] for l in range(2)]
    HA = [[nc.dram_tensor("ha%d%d" % (l, k), [2 * D, T // 2], BF16) for k in range(2)] for l in range(2)]
    MS = [[nc.dram_tensor("ms%d%d" % (l, k), [256, S_LEN], BF16) for k in range(2)] for l in range(2)]
    MA = [[nc.dram_tensor("ma%d%d" % (l, k), [2 * 256, S_LEN], BF16) for k in range(2)] for l in range(2)]

    def ag(S, src, dst, deps, writes=()):
        return S.add("pool", lambda e: e.collective_compute(
            "AllGather", ALU.bypass, replica_groups=RG_PAIRS, ins=[src.ap().opt()], outs=[dst.ap().opt()]),
            extra_deps=deps, cc=True, writes=list(writes))

    with contextlib.ExitStack() as st:
        sp = SemPool(nc, st)
        PS = st.enter_context(nc.psum_tensor("PS", [128, 8, 512], F32))
        xsp_ap = lambda kc: XSP[ts(kc, 128), :]
        for l in range(2):
            def h_dst(kc, ch, l=l):
                return HS[l][ch][ts(kc, 128), :]

            def h_tail(S, houts, l=l):
                ag(S, HS[l][0], HA[l][0], houts)

            def m_src(kc, half, l=l):
                grp, rk, sub = kc // 4, (kc // 2) % 2, kc % 2
                return MA[l][grp][rk * 256 + sub * 128: rk * 256 + (sub + 1) * 128, half * T:(half + 1) * T]
            if l == 0:
                emit_T(nc, sp, PS, lambda kc: xT[ts(kc, 128), :], xsp_ap, gains, sel, [0, 1],
                       ffnA=ffn[0][0], h_dst=h_dst, tail=h_tail)
            lam0 = 0.8 - 0.6 * math.exp(-0.3 * l)

            def hT_src(hf, kc, ch, l=l):
                return HA[l][ch][hf * D + kc * 128: hf * D + (kc + 1) * 128, :]

            def mix_dst(kind, i, l=l):
                return MS[l][0 if kind == "a" else 1][ts(i, 128), :]

            def tail2(S, outs, l=l):
                ag(S, MS[l][0], MA[l][0], list(outs))

            def tail3(S, outs, l=l):
                ag(S, MS[l][1], MA[l][1], list(outs))
            def head(S, l=l):
                ag(S, HS[l][1], HA[l][1], [], writes=[("HA", 1)])
            emit_B(nc, lam0, None, wB[l], strips, dtile, lamv[l], sgain[l], ident, None, sp=sp, final_wait=True, PS=PS,
                   hT_src=hT_src, tail=tail3, mix_dst=mix_dst, tail2=tail2, head=head)
            if l == 0:
                h_dst1 = lambda kc, ch: HS[1][ch][ts(kc, 128), :]

                def h_tail1(S, houts):
                    ag(S, HS[1][0], HA[1][0], houts)
                emit_T(nc, sp, PS, xsp_ap, xsp_ap, gains, sel, [2, 3, 4], ffnA=ffn[1][0], ffnB=ffn[0][1], wo=wo[0],
                       m_src=m_src, h_dst=h_dst1, tail=h_tail1)
            else:
                emit_T(nc, sp, PS, xsp_ap, lambda kc: xT_o[ts(kc, 128), :], gains, sel, [5, 6], ffnB=ffn[1][1],
                       wo=wo[1], m_src=m_src, final=True)
    return nc


def _ffn_layouts(Wg, Wu, Wd):
    wg_l = np.ascontiguousarray(Wg.reshape(KC, 128, NJ, 128).transpose(2, 1, 0, 3))
    wu_l = np.ascontiguousarray(Wu.reshape(KC, 128, NJ, 128).transpose(2, 1, 0, 3))
    wd_l = np.ascontiguousarray(Wd.reshape(NJ, 128, KC, 128).transpose(2, 1, 0, 3))
    return wg_l, wu_l, wd_l


_NC_CACHE = {}


def kernel(x, ffn1_norm, ffn1_w_gate, ffn1_w_up, ffn1_w_down, mix_norm, w_in,
           lambda_q1, lambda_k1, lambda_q2, lambda_k2, subln_gain, w_out,
           ffn2_norm, ffn2_w_gate, ffn2_w_up, ffn2_w_down, rel_bias, final_norm):
    f32 = np.float32
    A = lambda v: np.asarray(v, f32)
    x = A(x)
    cores = list(range(8))
    if "nc" not in _NC_CACHE:
        _NC_CACHE["nc"] = build_fused()
    nc = _NC_CACHE["nc"]
    gl = np.zeros((128, NG, KC), f32)
    for i, v in enumerate([A(ffn1_norm)[0], A(mix_norm)[0], A(ffn2_norm)[0], A(ffn1_norm)[1], A(mix_norm)[1],
                           A(ffn2_norm)[1], A(final_norm)]):
        gl[:, i, :] = v.reshape(KC, 128).T
    shared = dict(gains=gl, ident=np.eye(128, dtype=f32))
    for l in range(2):
        for k, (wg, wu, wd) in enumerate(((ffn1_w_gate, ffn1_w_up, ffn1_w_down), (ffn2_w_gate, ffn2_w_up, ffn2_w_down))):
            a, b, c_ = _ffn_layouts(A(wg)[l], A(wu)[l], A(wd)[l])
            shared["wg%d%d" % (l, k)], shared["wu%d%d" % (l, k)], shared["wd%d%d" % (l, k)] = a, b, c_
        shared["wo%d" % l] = np.ascontiguousarray(A(w_out)[l].reshape(KC, 128, KC, 128).transpose(2, 1, 0, 3))
        shared["lamv%d" % l] = np.ascontiguousarray(np.broadcast_to(
            np.stack([A(lambda_q1)[l], A(lambda_k1)[l], A(lambda_q2)[l], A(lambda_k2)[l]])[None], (128, 4, 64)))
        shared["sgain%d" % l] = np.ascontiguousarray(np.broadcast_to(A(subln_gain)[l][None], (128, 128)))
    per_g = []
    for g in range(2):
        st_, dt_ = prep_B_consts(A(rel_bias), g)
        d = dict(strips=st_, dtile=dt_)
        for l in range(2):
            d["wB%d" % l] = prep_B_weights(A(w_in)[l], g)
        s = np.zeros((128, 2), f32)
        s[:, g] = 1.0
        d["sel"] = s
        per_g.append(d)
    maps = []
    for c in cores:
        b, r = c // 2, c % 2
        m = dict(shared)
        m.update(per_g[r])
        m["xT"] = np.ascontiguousarray(x[b, r * T:(r + 1) * T, :].T)
        maps.append(m)
    res = run_bass_kernel_spmd(nc, maps, core_ids=cores)
    out = np.empty((4, 4096, D), f32)
    for c in cores:
        out[c // 2, (c % 2) * T:(c % 2 + 1) * T, :] = np.asarray(res.results[c]["xT_o"]).T
    return out
```

```python
import ml_dtypes
from concourse.bass_utils import run_bass_kernel_spmd
import contextlib
import concourse.bass as bass
import concourse.mybir as mybir

ENGS = ("pe", "act", "dve", "pool", "sp")
N_DMA_SEMS = {"sp": 12, "pool": 12, "act": 6}


class SemPool:
    def __init__(self, nc, stack):
        self.esem = {e: stack.enter_context(nc.semaphore("s_" + e)) for e in ENGS}
        self.dsem = {e: [stack.enter_context(nc.semaphore("d_%s%d" % (e, k))) for k in range(N_DMA_SEMS[e])]
                     for e in N_DMA_SEMS}
        self.ecnt = {e: 0 for e in ENGS}
        self.dcnt = {e: [0] * N_DMA_SEMS[e] for e in N_DMA_SEMS}
        self.dnext = {e: 0 for e in N_DMA_SEMS}
        self.csem = stack.enter_context(nc.semaphore("s_cc"))
        self.ccnt = 0
        self.stage = 0


class Sched:
    def __init__(self):
        self.ops = []
        self.res = {}

    def add(self, eng, fn, reads=(), writes=(), dma=False, extra_deps=(), cc=False):
        i = len(self.ops)
        deps = set(extra_deps)
        for k in reads:
            r = self.res.get(k)
            if r is not None and r[0] is not None:
                deps.add(r[0])
        for k in writes:
            r = self.res.get(k)
            if r is not None:
                if r[1]:
                    deps.update(r[1])
                elif r[0] is not None:
                    deps.add(r[0])
        for k in reads:
            self.res.setdefault(k, [None, []])[1].append(i)
        for k in writes:
            self.res[k] = [i, []]
        deps.discard(i)
        best = {}
        keep = set()
        for d in deps:
            od = self.ops[d]
            if od["dma"]:
                keep.add(d)
            else:
                b = best.get(od["eng"])
                if b is None or d > b:
                    best[od["eng"]] = d
        keep.update(best.values())
        self.ops.append(dict(eng=eng, fn=fn, deps=keep, dma=(dma or cc), sig=(dma or cc), cc=cc))
        return i

    def emit(self, nc, sp):
        ops = self.ops
        base_e = dict(sp.ecnt)
        dma_hist = {e: {} for e in N_DMA_SEMS}
        prev_cc = None
        base_cc = sp.ccnt
        for i, op in enumerate(ops):
            if op["cc"]:
                sp.ccnt += 1
                op["val"] = sp.ccnt
                prev_cc = i
                continue
            if op["dma"]:
                e = op["eng"]
                slot = sp.dnext[e]
                sp.dnext[e] = (slot + 1) % N_DMA_SEMS[e]
                op["slot"] = slot
                sp.dcnt[e][slot] += 16
                op["val"] = sp.dcnt[e][slot]
                if slot in dma_hist[e]:
                    op["deps"].add(dma_hist[e][slot])
                dma_hist[e][slot] = i
        for i, op in enumerate(ops):
            keep = set()
            for d in op["deps"]:
                od = ops[d]
                if (not od["dma"]) and od["eng"] == "pe" and op["eng"] == "pe" and not op["dma"]:
                    continue
                keep.add(d)
                od["sig"] = True
            op["deps"] = keep
        for e in ENGS:
            for op in reversed(ops):
                if op["eng"] == e and not op["dma"] and op["fn"] is not None:
                    op["sig"] = True
                    break
        for op in ops:
            if op["sig"] and not op["dma"]:
                sp.ecnt[op["eng"]] += 1
                op["val"] = sp.ecnt[op["eng"]]
        prev_e = base_e
        prev_d = {e: [sp.dcnt[e][k] for k in range(N_DMA_SEMS[e])] for e in N_DMA_SEMS}
        for op in ops:
            if op["dma"] and not op["cc"]:
                prev_d[op["eng"]][op["slot"]] -= 16

        def event(d):
            od = ops[d]
            if od["cc"]:
                return sp.csem, od["val"], ("c",)
            if od["dma"]:
                return sp.dsem[od["eng"]][od["slot"]], od["val"], ("d", od["eng"], od["slot"])
            return sp.esem[od["eng"]], od["val"], ("e", od["eng"])

        first_stage = sp.stage == 0
        sp.stage += 1
        with nc.Block() as block:
            def run(eng_name, e):
                waited = {}
                if not first_stage:
                    for e2 in ENGS:
                        if prev_e[e2] > 0 and e2 != eng_name:
                            e.wait_ge(sp.esem[e2], prev_e[e2])
                        waited[("e", e2)] = prev_e[e2]
                    for e2 in N_DMA_SEMS:
                        for k in range(N_DMA_SEMS[e2]):
                            if prev_d[e2][k] > 0:
                                e.wait_ge(sp.dsem[e2][k], prev_d[e2][k])
                            waited[("d", e2, k)] = prev_d[e2][k]
                    if base_cc > 0:
                        e.wait_ge(sp.csem, base_cc)
                    waited[("c",)] = base_cc
                for op in ops:
                    if op["eng"] != eng_name:
                        continue
                    need = {}
                    for d in op["deps"]:
                        s, v, key = event(d)
                        if waited.get(key, 0) >= v:
                            continue
                        if key not in need or need[key][1] < v:
                            need[key] = (s, v)
                    for key, (s, v) in need.items():
                        e.wait_ge(s, v)
                        waited[key] = v
                    if op["fn"] is None:
                        continue
                    ins = op["fn"](e)
                    if op["cc"]:
                        ins.then_inc(sp.csem)
                    elif op["dma"]:
                        ins.then_inc(sp.dsem[eng_name][op["slot"]], 16)
                    elif op["sig"]:
                        ins.then_inc(sp.esem[eng_name], 1)

            @block.tensor
            def _(e):
                run("pe", e)

            @block.scalar
            def _(e):
                run("act", e)

            @block.vector
            def _(e):
                run("dve", e)

            @block.gpsimd
            def _(e):
                run("pool", e)

            @block.sync
            def _(e):
                run("sp", e)


import numpy as np
import concourse.bass as bass
import concourse.mybir as mybir

F32 = mybir.dt.float32
BF16 = mybir.dt.bfloat16
AF = mybir.ActivationFunctionType
ALU = mybir.AluOpType

D = 1024
KC = 8
DFF = 2816
NJ = 22
T = 2048
TT = 512
NT = T // TT
EPS = 1e-6


def ts(i, n):
    return slice(i * n, (i + 1) * n)


class Ctx:
    pass


_UID = [0]


def _nm(name):
    return "%s_u%d" % (name, _UID[0])


def emit_norm(S, c, gkey, G32, on_tile=None, out=None):
    def A(tt):
        bank = 4 + tt
        for kc in range(KC):
            sb = (tt * KC + kc) % 2
            S.add("act", lambda e, kc=kc, tt=tt, sb=sb: e.activation(
                c.SQ[:, sb, :], c.XT[:, kc, ts(tt, TT)], AF.Square),
                reads=[("XT", kc, tt)], writes=[("SQ", sb)])
            S.add("pe", lambda e, kc=kc, sb=sb, bank=bank: e.matmul(
                c.PS[:, bank, :], c.ONES[:, :], c.SQ[:, sb, :], start=(kc == 0), stop=(kc == KC - 1)),
                reads=[("SQ", sb), ("ONES",)], writes=[("PS", bank)])

    def B(tt):
        bank = 4 + tt
        rb = tt % 2
        S.add("dve", lambda e, bank=bank, rb=rb: e.tensor_scalar(
            c.RSTD[:, rb, :], c.PS[:, bank, :], 1.0 / D, EPS, ALU.mult, ALU.add),
            reads=[("PS", bank)], writes=[("RSTD", rb)])
        S.add("dve", lambda e, rb=rb: e.reciprocal(c.RSTD[:, rb, :], c.RSTD[:, rb, :]),
              reads=[("RSTD", rb)], writes=[("RSTD", rb)])
        S.add("act", lambda e, rb=rb: e.activation(c.RSTD[:, rb, :], c.RSTD[:, rb, :], AF.Sqrt),
              reads=[("RSTD", rb)], writes=[("RSTD", rb)])

    def C(tt):
        rb = tt % 2
        for kc in range(KC):
            if out == "XT":
                S.add("dve", lambda e, kc=kc, tt=tt, rb=rb: e.scalar_tensor_tensor(
                    c.XT[:, kc, ts(tt, TT)], c.XT[:, kc, ts(tt, TT)], G32[:, kc:kc + 1], c.RSTD[:, rb, :],
                    ALU.mult, ALU.mult),
                    reads=[("XT", kc, tt), ("RSTD", rb), gkey], writes=[("XT", kc, tt)])
            else:
                S.add("dve", lambda e, kc=kc, tt=tt, rb=rb: e.scalar_tensor_tensor(
                    c.HT[:, kc, ts(tt, TT)], c.XT[:, kc, ts(tt, TT)], G32[:, kc:kc + 1], c.RSTD[:, rb, :],
                    ALU.mult, ALU.mult),
                    reads=[("XT", kc, tt), ("RSTD", rb), gkey], writes=[("HT", kc, tt)])
        if on_tile is not None:
            on_tile(tt)

    A(0)
    A(1)
    B(0)
    C(0)
    A(2)
    B(1)
    C(1)
    A(3)
    B(2)
    C(2)
    B(3)
    C(3)


def emit_ffn(S, c, wg, wu, wd, norm=None):
    def load_w(j):
        wb = j % 2
        S.add("pool", lambda e, j=j, wb=wb: e.dma_start(out=c.WG[wb], in_=wg[j]),
              writes=[("WG", wb)], dma=True)
        S.add("pool", lambda e, j=j, wb=wb: e.dma_start(out=c.WU[wb], in_=wu[j]),
              writes=[("WU", wb)], dma=True)

    cnt = [0]

    def up(j, tt):
        wb = j % 2
        pb = cnt[0] % 2
        cnt[0] += 1
        bg, bu = pb, 2 + pb
        for kc in range(KC):
            S.add("pe", lambda e, kc=kc, tt=tt, wb=wb, bg=bg: e.matmul(
                c.PS[:, bg, :], c.WG[wb][:, kc, :], c.HT[:, kc, ts(tt, TT)],
                start=(kc == 0), stop=(kc == KC - 1)),
                reads=[("WG", wb), ("HT", kc, tt)], writes=[("PS", bg)])
        for kc in range(KC):
            S.add("pe", lambda e, kc=kc, tt=tt, wb=wb, bu=bu: e.matmul(
                c.PS[:, bu, :], c.WU[wb][:, kc, :], c.HT[:, kc, ts(tt, TT)],
                start=(kc == 0), stop=(kc == KC - 1)),
                reads=[("WU", wb), ("HT", kc, tt)], writes=[("PS", bu)])
        S.add("act", lambda e, bg=bg, pb=pb: e.activation(c.SIL[:, pb, :], c.PS[:, bg, :], AF.Silu),
              reads=[("PS", bg)], writes=[("SIL", pb)])
        S.add("dve", lambda e, j=j, tt=tt, bu=bu, pb=pb: e.tensor_tensor(
            c.ACTT[:, j, ts(tt, TT)], c.SIL[:, pb, :], c.PS[:, bu, :], ALU.mult),
            reads=[("SIL", pb), ("PS", bu)], writes=[("ACTT", j, tt)])

    load_w(0)
    load_w(1)
    if norm is not None:
        emit_norm(S, c, norm[0], norm[1], on_tile=lambda tt: (up(0, tt), up(1, tt)))
    else:
        for tt in range(NT):
            up(0, tt)
            up(1, tt)
    for j in range(2, NJ):
        load_w(j)
        for tt in range(NT):
            up(j, tt)
    ALIAS = [("WG", 0), ("WG", 1), ("WU", 0)]
    for cc in range(KC):
        wb = cc % 2
        wkeys = [("WD", 0)] if wb == 0 else ALIAS
        S.add("pool", lambda e, cc=cc, wb=wb: e.dma_start(out=c.WD[wb], in_=wd[cc]),
              writes=wkeys, dma=True)
        for j in range(NJ):
            for tt in range(NT):
                bank = 4 * wb + tt
                S.add("pe", lambda e, j=j, tt=tt, wb=wb, bank=bank: e.matmul(
                    c.PS[:, bank, :], c.WD[wb][:, j, :], c.ACTT[:, j, ts(tt, TT)],
                    start=(j == 0), stop=(j == NJ - 1)),
                    reads=wkeys + [("ACTT", j, tt)], writes=[("PS", bank)])
        for tt in range(NT):
            bank = 4 * wb + tt
            S.add("dve", lambda e, cc=cc, tt=tt, bank=bank: e.scalar_tensor_tensor(
                c.XT[:, cc, ts(tt, TT)], c.PS[:, bank, :], 0.5, c.XT[:, cc, ts(tt, TT)],
                ALU.mult, ALU.add),
                reads=[("PS", bank), ("XT", cc, tt)], writes=[("XT", cc, tt)])


import math, contextlib, os
NSTAGE = int(os.environ.get('NSTAGE', '3'))
D3 = int(os.environ.get('D3', '9'))
import numpy as np
import concourse.bass as bass
import concourse.mybir as mybir

S_LEN = 4096
NB = 32
NQC = 8
WCOLS = 1536
STRIP = 2560
DSTRIP = 3072
DILS = (1, 4, 16)
NEG = -30000.0


def t5_bucket_np(dist):
    n = np.maximum(dist, 0)
    nf = np.maximum(n, 1).astype(np.float32)
    large = 16 + (np.log(nf / np.float32(16)) / np.float32(math.log(2048 / 16)) * np.float32(16)).astype(np.int32)
    large = np.minimum(large, 31)
    return np.where(n < 16, n, large)


def prep_B_consts(rel_bias, g):
    p = np.arange(128)[:, None]
    c = np.arange(STRIP)[None, :]
    dd = c - p - 384
    bk = t5_bucket_np(dd)
    strips = np.empty((128, 4, STRIP), np.float32)
    for hh in range(2):
        for m in range(2):
            col = (2 * g + hh) * 2 + m
            strips[:, hh * 2 + m, :] = np.where(dd >= 0, rel_bias[bk, col], np.float32(NEG))
    c = np.arange(DSTRIP)[None, :]
    dd = c - p - 384
    bk = t5_bucket_np(dd)
    dtile = np.empty((128, 5, DSTRIP), np.float32)
    for h in range(4):
        dtile[:, h, :] = rel_bias[bk, 8 + 4 * g + h]
    ddc = np.maximum(dd, 0)
    cnt = ((ddc <= 128).astype(np.float32) + ((ddc <= 512) & (ddc % 4 == 0)).astype(np.float32)
           + ((ddc <= 2048) & (ddc % 16 == 0)).astype(np.float32))
    dtile[:, 4, :] = np.where(dd >= 0, cnt, 0.0)
    return strips, dtile


def prep_B_weights(w_in_l, g):
    cols = []
    for base in (0, 512, 1024, 1536, 2048, 2560):
        cols.append(w_in_l[:, base + g * 256: base + (g + 1) * 256])
    w = np.concatenate(cols, axis=1)
    return np.ascontiguousarray(w.reshape(8, 128, WCOLS).transpose(1, 0, 2))


def build_B(layer):
    lam0 = 0.8 - 0.6 * math.exp(-0.3 * layer)
    nc = bass.Bass("TRN2", target_bir_lowering=False)
    hT = nc.dram_tensor("hT", [2, 1024, 2048], BF16, kind="ExternalInput").ap()
    w = nc.dram_tensor("w", [128, 8, WCOLS], F32, kind="ExternalInput").ap()
    strips = nc.dram_tensor("strips", [128, 4, STRIP], F32, kind="ExternalInput").ap()
    dtile = nc.dram_tensor("dtile", [128, 5, DSTRIP], F32, kind="ExternalInput").ap()
    lamv = nc.dram_tensor("lamv", [128, 4, 64], F32, kind="ExternalInput").ap()
    sgain = nc.dram_tensor("sgain", [128, 128], F32, kind="ExternalInput").ap()
    ident = nc.dram_tensor("ident", [128, 128], F32, kind="ExternalInput").ap()
    mixT = nc.dram_tensor("mixT", [512, S_LEN], BF16, kind="ExternalOutput").ap()
    emit_B(nc, lam0, hT, w, strips, dtile, lamv, sgain, ident, mixT)
    return nc


def emit_B(nc, lam0, hT, w, strips, dtile, lamv, sgain, ident, mixT, sp=None, final_wait=True, PS=None,
           hT_src=None, tail=None, mix_dst=None, tail2=None, head=None, b8_store=None, b8_load=None):
    _UID[0] += 1
    with contextlib.ExitStack() as st:
        def sb(name, shape, dt):
            return st.enter_context(nc.sbuf_tensor(_nm(name), shape, dt))
        if sp is None:
            sp = SemPool(nc, st)
        if PS is None:
            PS = st.enter_context(nc.psum_tensor("PSB", [128, 8, 512], F32))
        if hT_src is None:
            hT_src = lambda hf, kc, ch: hT[hf, ts(kc, 128), ch * 1024:(ch + 1) * 1024]
        if mix_dst is None:
            mix_dst = lambda kind, i: (mixT[ts(i, 128), :] if kind == "a" else mixT[256 + i * 128: 256 + (i + 1) * 128, :])
        QBT = sb("QBT", [128, 2, S_LEN], BF16)
        KBT = sb("KBT", [128, 2, S_LEN], BF16)
        VB = sb("VB", [128, 1, NB, 4, 65], BF16)
        IDN = sb("IDN", [128, 128], BF16)
        ONE32 = sb("ONE32", [128, 64], F32)
        LAM = sb("LAM", [128, 8], F32)
        SG = sb("SG", [128, 128], F32)
        sd = contextlib.ExitStack()
        QAT = sd.enter_context(nc.sbuf_tensor(_nm("QAT"), [128, 2, S_LEN], BF16))
        KAT = sd.enter_context(nc.sbuf_tensor(_nm("KAT"), [128, 2, S_LEN], BF16))
        VA = sd.enter_context(nc.sbuf_tensor(_nm("VA"), [128, NB, 2, 129], BF16))
        with contextlib.ExitStack() as s1:
            def sb1(name, shape, dt):
                return s1.enter_context(nc.sbuf_tensor(_nm(name), shape, dt))
            S = Sched()
            HTs = sb1("HTs", [128, 2, 8, 2048], BF16)
            W = sb1("W", [128, 8, WCOLS], BF16)
            LV = sb1("LV", [128, 4, 64], F32)
            hpc = [0]
            if head is not None:
                head(S)
            S.add("pool", lambda e: e.dma_start(out=W[:, :, :], in_=w), writes=[("W",)], dma=True)
            S.add("pool", lambda e: e.dma_start(out=IDN[:, :], in_=ident), writes=[("IDN",)], dma=True)
            S.add("sp", lambda e: e.dma_start(out=LV[:, :, :], in_=lamv), writes=[("LV",)], dma=True)
            S.add("sp", lambda e: e.dma_start(out=SG[:, :], in_=sgain), writes=[("SG",)], dma=True)
            S.add("dve", lambda e: e.memset(ONE32[:, :], 1.0), writes=[("ONE32",)])
            S.add("pool", lambda e: e.memset(VA[:, :, :, 128:129], 1.0), writes=[("VAones",)])
            S.add("pool", lambda e: e.memset(VB[:, :, :, :, 64:65], 1.0), writes=[("VBones",)])
            S.add("dve", lambda e: e.tensor_tensor(LV[:, 0, :], LV[:, 0, :], LV[:, 1, :], ALU.mult),
                  reads=[("LV",)], writes=[("LV",)])
            S.add("dve", lambda e: e.tensor_tensor(LV[:, 2, :], LV[:, 2, :], LV[:, 3, :], ALU.mult),
                  reads=[("LV",)], writes=[("LV",)])
            S.add("dve", lambda e: e.tensor_reduce(LAM[:, 2:3], LV[:, 0, :], mybir.AxisListType.X, ALU.add),
                  reads=[("LV",)], writes=[("LAM",)])
            S.add("dve", lambda e: e.tensor_reduce(LAM[:, 3:4], LV[:, 2, :], mybir.AxisListType.X, ALU.add),
                  reads=[("LV",)], writes=[("LAM",)])
            S.add("act", lambda e: e.activation(LAM[:, 4:6], LAM[:, 2:4], AF.Exp), reads=[("LAM",)], writes=[("LAM",)])
            S.add("dve", lambda e: e.tensor_tensor(LAM[:, 0:1], LAM[:, 4:5], LAM[:, 5:6], ALU.subtract),
                  reads=[("LAM",)], writes=[("LAM",)])
            S.add("dve", lambda e: e.tensor_scalar(LAM[:, 0:1], LAM[:, 0:1], float(lam0), None, ALU.add),
                  reads=[("LAM",)], writes=[("LAM",)])
            S.add("dve", lambda e: e.tensor_scalar(LAM[:, 1:2], LAM[:, 0:1], -1.0, None, ALU.mult),
                  reads=[("LAM",)], writes=[("LAM",)])
            S.add("dve", lambda e: e.tensor_scalar(SG[:, :], SG[:, :], float(1.0 - lam0), None, ALU.mult),
                  reads=[("SG",)], writes=[("SG",)])
            cp = [0]

            def evac(out_ap, in_ap, reads, writes):
                eng = "act" if cp[0] % 2 == 0 else "dve"
                cp[0] += 1
                if eng == "act":
                    S.add("act", lambda e: e.copy(out_ap, in_ap), reads=reads, writes=writes)
                else:
                    S.add("dve", lambda e: e.tensor_copy(out_ap, in_ap), reads=reads, writes=writes)

            pbank = [0]

            def nextbank():
                b = pbank[0]
                pbank[0] = (b + 1) % 8
                return b

            for ch in range(2):
                for hf in range(2):
                    for kc in range(8):
                        S.add("sp" if kc % 2 == 0 else "act", lambda e, kc=kc, hf=hf, ch=ch: e.dma_start(
                            out=HTs[:, hf, kc, ch * 1024:(ch + 1) * 1024], in_=hT_src(hf, kc, ch)),
                            reads=[("HA", ch)], writes=[("HTs", hf, kc, ch)], dma=True)
            for ch, hf, tt in [(ch, hf, ch * 2 + t2) for ch in range(2) for hf in range(2) for t2 in range(2)]:
                if True:
                    qc = hf * 4 + tt
                    for (dst, di, c0) in ((QAT, 0, 0), (QAT, 1, 128), (KAT, 0, 256), (KAT, 1, 384),
                                          (QBT, 0, 768), (QBT, 1, 896), (KBT, 0, 1024), (KBT, 1, 1152)):
                        b = nextbank()
                        for kc in range(8):
                            S.add("pe", lambda e, kc=kc, tt=tt, c0=c0, b=b, hf=hf: e.matmul(
                                PS[:, b, :], W[:, kc, c0:c0 + 128], HTs[:, hf, kc, ts(tt, 512)],
                                start=(kc == 0), stop=(kc == 7)),
                                reads=[("W",), ("HTs", hf, kc, ch)], writes=[("PS", b)])
                        evac(dst[:, di, ts(qc, 512)], PS[:, b, :], [("PS", b)], [("QK", id(dst), di, qc)])
                    for bl in range(4):
                        blk = qc * 4 + bl
                        b = nextbank()
                        for kc in range(8):
                            S.add("pe", lambda e, kc=kc, tt=tt, bl=bl, b=b, hf=hf: e.matmul(
                                PS[:, b, 0:256], HTs[:, hf, kc, tt * 512 + bl * 128: tt * 512 + (bl + 1) * 128],
                                W[:, kc, 512:768], start=(kc == 0), stop=(kc == 7)),
                                reads=[("W",), ("HTs", hf, kc, ch)], writes=[("PS", b)])
                        evac(VA[:, blk, :, 0:128], PS[:, b, 0:256].rearrange("p (h d) -> p h d", h=2),
                             [("PS", b), ("VAones",)], [("VA", blk)])
                        b = nextbank()
                        for kc in range(8):
                            S.add("pe", lambda e, kc=kc, tt=tt, bl=bl, b=b, hf=hf: e.matmul(
                                PS[:, b, 0:256], HTs[:, hf, kc, tt * 512 + bl * 128: tt * 512 + (bl + 1) * 128],
                                W[:, kc, 1280:1536], start=(kc == 0), stop=(kc == 7)),
                                reads=[("W",), ("HTs", hf, kc, ch)], writes=[("PS", b)])
                        evac(VB[:, 0, blk, :, 0:64], PS[:, b, 0:256].rearrange("p (h d) -> p h d", h=4),
                             [("PS", b), ("VBones",)], [("VB", 0, blk)])
            S.emit(nc, sp)
        def attn(S, sbx, kind, Qt, Kt, vfn, B8, npair, outs, mixname):
            W = 129 if kind == "a" else 65
            PT = sbx("PT" + kind, [128, int(os.environ.get("AVLAG", "2")) + 2, 2, 512], BF16)
            OS = sbx("OS" + kind, [128, 2, 8, W], F32)
            OA = sbx("OA" + kind, [128, 4, 128], F32)
            SQ = sbx("SQ" + kind, [128, 4, 128], F32)
            OAN = sbx("OAN" + kind, [128, 2, 4, 128], BF16)
            RC = sbx("RC" + kind, [128, 2, 12], F32)
            MIX = sbx(mixname, [128, 2, S_LEN], BF16)

            def oloc(i, sbi):
                if kind == "a":
                    idx = i * 4 + sbi
                    return 4 + idx // 3, (idx % 3) * W, (idx % 3 == 0)
                return 4 + i, sbi * W, (sbi == 0)
            tiles = []
            for pair in range(npair):
                for qc in range(NQC):
                    kb0 = 0 if kind == "a" else max(0, 4 * qc - 16)
                    for kb in range(kb0, 4 * qc + 4):
                        tiles.append((pair, qc, kb, kb0))
            AVLAG = int(os.environ.get('AVLAG', '2'))
            NPT = AVLAG + 2

            def cols(t):
                pair, qc, kb, kb0 = tiles[t]
                s_lo = max(0, kb - 4 * qc)
                s_hi = 3 if kind == "a" else min(3, 16 + kb - 4 * qc)
                return s_lo * 128, (s_hi + 1) * 128

            def emit_qk(t):
                pair, qc, kb, kb0 = tiles[t]
                sb0 = 2 * (t % 2)
                lo, hi = cols(t)
                for i in range(2):
                    S.add("pe", lambda e, i=i, pair=pair, kb=kb, qc=qc, sb0=sb0, lo=lo, hi=hi: e.matmul(
                        PS[:, sb0 + i, lo:hi], Kt[i * 64:(i + 1) * 64, pair, ts(kb, 128)],
                        Qt[i * 64:(i + 1) * 64, pair, qc * 512 + lo:qc * 512 + hi], start=True, stop=True),
                        writes=[("PS", sb0 + i)])

            def emit_act(t):
                pair, qc, kb, kb0 = tiles[t]
                sb0 = 2 * (t % 2)
                pt = t % NPT
                lo, hi = cols(t)
                delta = 512 * qc - 128 * kb
                c0 = min(delta + 384, 2048) if kind == "a" else delta + 384
                S.add("act", lambda e, sb0=sb0, pt=pt, lo=lo, hi=hi: e.activation(
                    PT[:, pt, :, lo:hi], PS[:, sb0:sb0 + 2, lo:hi], AF.Exp, scale=0.125),
                    reads=[("PS", sb0), ("PS", sb0 + 1)], writes=[("PT", pt)])
                S.add("dve", lambda e, pt=pt, pair=pair, c0=c0, lo=lo, hi=hi: e.tensor_tensor(
                    PT[:, pt, :, lo:hi], PT[:, pt, :, lo:hi], B8[:, pair * 2:pair * 2 + 2, c0 + lo:c0 + hi], ALU.mult),
                    reads=[("PT", pt), ("B8", pair * 2), ("B8", pair * 2 + 1)], writes=[("PT", pt)])

            def av_list(pair, qc, kb, kb0):
                out = []
                for i in range(2):
                    for sbi in range(4):
                        qi = 4 * qc + sbi
                        if kb > qi or (kind == "b" and qi - kb > 16):
                            continue
                        out.append((i, sbi))
                return out

            last_in_bank = {}
            for (pair, qc, kb, kb0) in tiles:
                for (i, sbi) in av_list(pair, qc, kb, kb0):
                    last_in_bank[(pair, qc, oloc(i, sbi)[0])] = (kb, i, sbi)

            def emit_av(t):
                pair, qc, kb, kb0 = tiles[t]
                pt = t % NPT
                for (i, sbi) in av_list(pair, qc, kb, kb0):
                    ob, oc, first = oloc(i, sbi)
                    S.add("pe", lambda e, pt=pt, i=i, sbi=sbi, kb=kb, pair=pair, ob=ob, oc=oc,
                          st_=(kb == kb0 and first), sp_=(last_in_bank[(pair, qc, ob)] == (kb, i, sbi)): e.matmul(
                        PS[:, ob, oc:oc + W], PT[:, pt, i, ts(sbi, 128)], vfn(kb, pair, i),
                        start=st_, stop=sp_),
                        reads=[("PT", pt)], writes=[("PSO", ob)])

            fin = 0
            pending = []
            KPOP = int(os.environ.get('KPOP', '1'))

            def DF(*a_, **k_):
                pending.append(lambda: S.add(*a_, **k_))

            def DFO(*a_, **k_):
                pending.append(lambda: outs.append(S.add(*a_, **k_)))
            ntl = len(tiles)
            emit_qk(0)
            for t in range(ntl + AVLAG):
                if t + 1 < ntl:
                    emit_qk(t + 1)
                if t < ntl:
                    emit_act(t)
                for _ in range(KPOP):
                    if pending:
                        pending.pop(0)()
                ta = t - AVLAG
                if ta < 0:
                    continue
                emit_av(ta)
                pair, qc, kb, kb0 = tiles[ta]
                if kb == 4 * qc + 3:
                    while pending:
                        pending.pop(0)()
                    f = fin % 2
                    fin += 1
                    if kind == "a":
                        for bnk, n in ((4, 3), (5, 3), (6, 2)):
                            eng = "act" if bnk == 5 else "dve"
                            dst = OS[:, f, (bnk - 4) * 3:(bnk - 4) * 3 + n, :]
                            src = PS[:, bnk, 0:n * W].rearrange("p (a w) -> p a w", w=W)
                            if eng == "act":
                                S.add("act", lambda e, dst=dst, src=src: e.copy(dst, src),
                                      reads=[("PSO", bnk)], writes=[("OS", f, bnk), ("PSO", bnk)])
                            else:
                                S.add("dve", lambda e, dst=dst, src=src: e.tensor_copy(dst, src),
                                      reads=[("PSO", bnk)], writes=[("OS", f, bnk), ("PSO", bnk)])
                        oskeys = [("OS", f, 4), ("OS", f, 5), ("OS", f, 6)]
                    else:
                        for i in range(2):
                            dst = OS[:, f, i * 4:(i + 1) * 4, :]
                            src = PS[:, 4 + i, 0:4 * W].rearrange("p (a w) -> p a w", w=W)
                            if i == 0:
                                S.add("act", lambda e, dst=dst, src=src: e.copy(dst, src),
                                      reads=[("PSO", 4 + i)], writes=[("OS", f, 4 + i), ("PSO", 4 + i)])
                            else:
                                S.add("dve", lambda e, dst=dst, src=src: e.tensor_copy(dst, src),
                                      reads=[("PSO", 4 + i)], writes=[("OS", f, 4 + i), ("PSO", 4 + i)])
                        oskeys = [("OS", f, 4), ("OS", f, 5)]
                    DF("dve", lambda e, f=f: e.reciprocal(RC[:, f, 0:8], OS[:, f, :, W - 1]),
                          reads=oskeys, writes=[("RC", f)])
                    if kind == "a":
                        DF("dve", lambda e, f=f: e.tensor_scalar(RC[:, f, 4:8], RC[:, f, 4:8], LAM[:, 1:2], None, ALU.mult),
                              reads=[("RC", f)], writes=[("RC", f)])
                        for sbi in range(4):
                            DF("dve", lambda e, f=f, sbi=sbi: e.tensor_scalar(
                                OA[:, sbi, :], OS[:, f, sbi, 0:128], RC[:, f, sbi:sbi + 1], None, ALU.mult),
                                reads=oskeys + [("RC", f)], writes=[("OA", sbi)])
                            DF("dve", lambda e, f=f, sbi=sbi: e.scalar_tensor_tensor(
                                OA[:, sbi, :], OS[:, f, 4 + sbi, 0:128], RC[:, f, 4 + sbi:5 + sbi], OA[:, sbi, :],
                                ALU.mult, ALU.add),
                                reads=oskeys + [("RC", f), ("OA", sbi)], writes=[("OA", sbi)])
                        DF("pool", lambda e: e.tensor_tensor(SQ[:, :, :], OA[:, :, :], OA[:, :, :], ALU.mult),
                              reads=[("OA", k) for k in range(4)], writes=[("SQ",)])
                        DF("dve", lambda e, f=f: e.tensor_reduce(RC[:, f, 8:12], SQ[:, :, :], mybir.AxisListType.X, ALU.add),
                              reads=[("SQ",)], writes=[("RC", f)])
                        DF("dve", lambda e, f=f: e.tensor_scalar(
                            RC[:, f, 8:12], RC[:, f, 8:12], 1.0 / 128, EPS, ALU.mult, ALU.add),
                            reads=[("RC", f)], writes=[("RC", f)])
                        DF("dve", lambda e, f=f: e.reciprocal(RC[:, f, 8:12], RC[:, f, 8:12]),
                              reads=[("RC", f)], writes=[("RC", f)])
                        DF("act", lambda e, f=f: e.activation(RC[:, f, 8:12], RC[:, f, 8:12], AF.Sqrt),
                              reads=[("RC", f)], writes=[("RC", f)])
                        for sbi in range(4):
                            DF("dve", lambda e, f=f, sbi=sbi: e.scalar_tensor_tensor(
                                OAN[:, f, sbi, :], OA[:, sbi, :], RC[:, f, 8 + sbi:9 + sbi], SG[:, :], ALU.mult, ALU.mult),
                                reads=[("OA", sbi), ("RC", f)], writes=[("OAN", f, sbi)])
                    else:
                        for sbi in range(4):
                            for i in range(2):
                                DF("dve", lambda e, f=f, sbi=sbi, i=i: e.tensor_scalar(
                                    OAN[:, f, sbi, i * 64:(i + 1) * 64], OS[:, f, i * 4 + sbi, 0:64],
                                    RC[:, f, i * 4 + sbi:i * 4 + sbi + 1], None, ALU.mult),
                                    reads=oskeys + [("RC", f)], writes=[("OAN", f, sbi)])
                    if True:
                        for sbi in range(4):
                            DF("pe", lambda e, f=f, sbi=sbi: e.transpose(
                                PS[:, 7, :].bitcast(BF16)[:, sbi * 128:(sbi + 1) * 128], OAN[:, f, sbi, :], IDN[:, :]),
                                reads=[("OAN", f, sbi)], writes=[("PS", 7)])
                        DF("act", lambda e, pair=pair, qc=qc: e.copy(
                            MIX[:, pair, ts(qc, 512)], PS[:, 7, :].bitcast(BF16)[:, 0:512]),
                            reads=[("PS", 7)], writes=[("MIX", pair, qc)])
                        if qc == NQC - 1:
                            DFO("sp", lambda e, pair=pair: e.dma_start(
                                out=mix_dst(kind, pair), in_=MIX[:, pair, :]),
                                reads=[("MIX", pair, q) for q in range(NQC)], dma=True)
            while pending:
                pending.pop(0)()

        outs = []
        if NSTAGE < 2:
            sd.close()
            return
        with contextlib.ExitStack() as s2:
            def sb2(name, shape, dt):
                return s2.enter_context(nc.sbuf_tensor(_nm(name), shape, dt))
            S = Sched()
            B8 = sb2("B8A", [128, 4, STRIP], BF16)
            STG = sb2("STG2", [128, 2, STRIP], F32)
            if b8_load is not None:
                for m in range(4):
                    S.add("sp" if m % 2 == 0 else "act", lambda e, m=m: e.dma_start(out=B8[:, m, :], in_=b8_load[0][:, m, :]),
                          writes=[("B8", m)], dma=True)
            else:
                for m in range(4):
                    S.add("sp" if m % 2 == 0 else "act", lambda e, m=m: e.dma_start(out=STG[:, m % 2, :], in_=strips[:, m, :]),
                          writes=[("STG", m % 2)], dma=True)
                    S.add("act", lambda e, m=m: e.activation(B8[:, m, :], STG[:, m % 2, :], AF.Exp),
                          reads=[("STG", m % 2)], writes=[("B8", m)])
                if b8_store is not None:
                    for m in range(4):
                        S.add("sp", lambda e, m=m: e.dma_start(out=b8_store[0][:, m, :], in_=B8[:, m, :]),
                              reads=[("B8", m)], dma=True)
            attn(S, sb2, "a", QAT, KAT, lambda kb, pair, i: VA[:, kb, pair, :], B8, 2, outs, "MIXA")
            if tail2 is not None:
                tail2(S, outs)
            S.add("sp", None, extra_deps=outs)
            S.emit(nc, sp)
        sd.close()
        if NSTAGE < 3:
            return
        with contextlib.ExitStack() as s3:
            def sb3(name, shape, dt):
                return s3.enter_context(nc.sbuf_tensor(_nm(name), shape, dt))
            S = Sched()
            B8 = sb3("B8B", [128, 4, DSTRIP], BF16)
            STG = sb3("STG3", [128, 2, DSTRIP], F32)
            CNT = sb3("CNT", [128, DSTRIP], F32)
            if b8_load is not None:
                for h in range(4):
                    S.add("sp" if h % 2 == 0 else "act", lambda e, h=h: e.dma_start(out=B8[:, h, :], in_=b8_load[1][:, h, :]),
                          writes=[("B8", h)], dma=True)
            else:
                S.add("pool", lambda e: e.dma_start(out=CNT[:, :], in_=dtile[:, 4, :]), writes=[("CNT",)], dma=True)
                for h in range(4):
                    S.add("sp" if h % 2 == 0 else "act", lambda e, h=h: e.dma_start(out=STG[:, h % 2, :], in_=dtile[:, h, :]),
                          writes=[("STG", h % 2)], dma=True)
                    S.add("act", lambda e, h=h: e.activation(STG[:, h % 2, :], STG[:, h % 2, :], AF.Exp),
                          reads=[("STG", h % 2)], writes=[("STG", h % 2)])
                    S.add("dve", lambda e, h=h: e.tensor_tensor(B8[:, h, :], STG[:, h % 2, :], CNT[:, :], ALU.mult),
                        reads=[("STG", h % 2), ("CNT",)], writes=[("B8", h)])
                if b8_store is not None:
                    for h in range(4):
                        S.add("sp", lambda e, h=h: e.dma_start(out=b8_store[1][:, h, :], in_=B8[:, h, :]),
                              reads=[("B8", h)], dma=True)
            outs = []
            attn(S, sb3, "b", QBT, KBT, lambda kb, pair, i: VB[:, 0, kb, 2 * pair + i, :], B8, 2, outs, "MIXB")
            if tail is not None:
                tail(S, outs)
            if final_wait:
                S.add("sp", None, extra_deps=outs)
            S.emit(nc, sp)


RG_PAIRS = [[0, 1], [2, 3], [4, 5], [6, 7]]
NG = 7


def emit_wout(S, c, wo):
    for cc in range(KC):
        wb = cc % 2
        S.add("pool", lambda e, cc=cc, wb=wb: e.dma_start(out=c.WG[wb], in_=wo[cc]), writes=[("WG", wb)], dma=True)
        for tt in range(NT):
            bank = 4 * wb + tt
            for kc in range(KC):
                S.add("pe", lambda e, kc=kc, tt=tt, wb=wb, bank=bank: e.matmul(
                    c.PS[:, bank, :], c.WG[wb][:, kc, :], c.HT[:, kc, ts(tt, TT)],
                    start=(kc == 0), stop=(kc == KC - 1)),
                    reads=[("WG", wb), ("HT", kc, tt)], writes=[("PS", bank)])
            S.add("dve", lambda e, cc=cc, tt=tt, bank=bank: e.tensor_tensor(
                c.XT[:, cc, ts(tt, TT)], c.XT[:, cc, ts(tt, TT)], c.PS[:, bank, :], ALU.add),
                reads=[("PS", bank), ("XT", cc, tt)], writes=[("XT", cc, tt)])


def emit_final_norm(S, c, gkey, G):
    for tt in range(NT):
        bank = 6 + (tt % 2)
        for kc in range(KC):
            sb = (tt * KC + kc) % 2
            S.add("act", lambda e, kc=kc, tt=tt, sb=sb: e.activation(
                c.SQ[:, sb, :], c.XT[:, kc, ts(tt, TT)], AF.Square),
                reads=[("XT", kc, tt)], writes=[("SQ", sb)])
            S.add("pe", lambda e, kc=kc, sb=sb, bank=bank: e.matmul(
                c.PS[:, bank, :], c.ONES[:, :], c.SQ[:, sb, :], start=(kc == 0), stop=(kc == KC - 1)),
                reads=[("SQ", sb)], writes=[("PS", bank)])
        rb = tt % 2
        S.add("dve", lambda e, bank=bank, rb=rb: e.tensor_scalar(
            c.RSTD[:, rb, :], c.PS[:, bank, :], 1.0 / D, EPS, ALU.mult, ALU.add),
            reads=[("PS", bank)], writes=[("RSTD", rb)])
        S.add("dve", lambda e, rb=rb: e.reciprocal(c.RSTD[:, rb, :], c.RSTD[:, rb, :]),
              reads=[("RSTD", rb)], writes=[("RSTD", rb)])
        S.add("act", lambda e, rb=rb: e.activation(c.RSTD[:, rb, :], c.RSTD[:, rb, :], AF.Sqrt),
              reads=[("RSTD", rb)], writes=[("RSTD", rb)])
        for kc in range(KC):
            S.add("dve", lambda e, kc=kc, tt=tt, rb=rb: e.scalar_tensor_tensor(
                c.XT[:, kc, ts(tt, TT)], c.XT[:, kc, ts(tt, TT)], G[:, kc:kc + 1], c.RSTD[:, rb, :],
                ALU.mult, ALU.mult),
                reads=[("XT", kc, tt), ("RSTD", rb), gkey], writes=[("XT", kc, tt)])


def emit_T(nc, sp, PS, x_src, x_dst, gains, sel, gidx, ffnA=None, ffnB=None, wo=None, m_src=None,
           h_dst=None, final=False, tail=None):
    _UID[0] += 1
    S = Sched()
    c = Ctx()
    c.PS = PS
    with contextlib.ExitStack() as st:
        def sb(name, shape, dt):
            return st.enter_context(nc.sbuf_tensor(_nm(name), shape, dt))
        c.XT = sb("XT", [128, KC, T], F32)
        c.HT = sb("HT", [128, KC, T], BF16)
        c.ACTT = sb("ACTT", [128, NJ, T], BF16)
        SCR = sb("SCR", [128, 4096], BF16)
        c.WG = [SCR[:, 0:1024].rearrange("p (k n) -> p k n", k=KC), SCR[:, 1024:2048].rearrange("p (k n) -> p k n", k=KC)]
        c.WU = [SCR[:, 2048:3072].rearrange("p (k n) -> p k n", k=KC), SCR[:, 3072:4096].rearrange("p (k n) -> p k n", k=KC)]
        WD0 = sb("WD0", [128, NJ, 128], BF16)
        c.WD = [WD0[:, :, :], SCR[:, 0:NJ * 128].rearrange("p (j n) -> p j n", j=NJ)]
        c.SQ = sb("SQ", [128, 2, TT], BF16)
        c.RSTD = sb("RSTD", [128, 2, TT], F32)
        c.SIL = sb("SIL", [128, 2, TT], BF16)
        c.ONES = sb("ONES", [128, 128], BF16)
        G = sb("G", [128, NG, KC], F32)
        SEL = sb("SEL", [128, 2], F32)
        S.add("dve", lambda e: e.memset(c.ONES[:, :], 1.0), writes=[("ONES",)])
        S.add("sp", lambda e: e.dma_start(out=G[:, :, :], in_=gains), writes=[("G",)], dma=True)
        S.add("sp", lambda e: e.dma_start(out=SEL[:, :], in_=sel), writes=[("SEL",)], dma=True)
        def load_x():
            for kc in range(KC):
                S.add("sp", lambda e, kc=kc: e.dma_start(out=c.XT[:, kc, :], in_=x_src(kc)),
                      writes=[("XT", kc, tt) for tt in range(NT)], dma=True)
        gi = iter(gidx)
        if wo is None:
            load_x()
        if wo is not None:
            for kc in range(KC):
                S.add("sp", lambda e, kc=kc: e.dma_start(out=c.HT[:, kc, :], in_=m_src(kc, 0)),
                      writes=[("HT", kc, tt) for tt in range(NT)], dma=True)
                S.add("act", lambda e, kc=kc: e.dma_start(out=c.ACTT[:, kc, :], in_=m_src(kc, 1)),
                      writes=[("ACTT", kc, tt) for tt in range(NT)], dma=True)
            load_x()
            for kc in range(KC):
                S.add("act", lambda e, kc=kc: e.activation(
                    c.HT[:, kc, :], c.HT[:, kc, :], AF.Copy, scale=SEL[:, 0:1]),
                    reads=[("SEL",)] + [("HT", kc, tt) for tt in range(NT)], writes=[("HT", kc, tt) for tt in range(NT)])
                S.add("dve", lambda e, kc=kc: e.scalar_tensor_tensor(
                    c.HT[:, kc, :], c.ACTT[:, kc, :], SEL[:, 1:2], c.HT[:, kc, :], ALU.mult, ALU.add),
                    reads=[("SEL",)] + [("HT", kc, tt) for tt in range(NT)] + [("ACTT", kc, tt) for tt in range(NT)],
                    writes=[("HT", kc, tt) for tt in range(NT)])
            emit_wout(S, c, wo)
            g = next(gi)
            emit_ffn(S, c, *ffnB, norm=(("G",), G[:, g, :]))
        outs = []
        houts = []
        if ffnA is not None:
            g = next(gi)
            emit_ffn(S, c, *ffnA, norm=(("G",), G[:, g, :]))
            g = next(gi)
            emit_norm(S, c, ("G",), G[:, g, :])
            for ch in range(2):
                for kc in range(KC):
                    houts.append(S.add("sp", lambda e, kc=kc, ch=ch: e.dma_start(
                        out=h_dst(kc, ch), in_=c.HT[:, kc, ch * (T // 2):(ch + 1) * (T // 2)]),
                        reads=[("HT", kc, 2 * ch), ("HT", kc, 2 * ch + 1)], dma=True))
                if ch == 0 and tail is not None:
                    tail(S, list(houts))
        if final:
            g = next(gi)
            emit_norm(S, c, ("G",), G[:, g, :], out="XT")
        for kc in range(KC):
            outs.append(S.add("sp", lambda e, kc=kc: e.dma_start(out=x_dst(kc), in_=c.XT[:, kc, :]),
                              reads=[("XT", kc, tt) for tt in range(NT)], dma=True))
        S.add("sp", None, extra_deps=outs + houts)
        S.emit(nc, sp)


def build_fused():
    nc = bass.Bass("TRN2", target_bir_lowering=False)

    def din(name, shape, dt=F32):
        return nc.dram_tensor(name, shape, dt, kind="ExternalInput").ap()
    xT = din("xT", [D, T])
    gains = din("gains", [128, NG, KC])
    sel = din("sel", [128, 2])
    ffn = [[(din("wg%d%d" % (l, k), [NJ, 128, KC, 128]), din("wu%d%d" % (l, k), [NJ, 128, KC, 128]),
             din("wd%d%d" % (l, k), [KC, 128, NJ, 128])) for k in range(2)] for l in range(2)]
    wo = [din("wo%d" % l, [KC, 128, KC, 128]) for l in range(2)]
    wB = [din("wB%d" % l, [128, 8, WCOLS]) for l in range(2)]
    lamv = [din("lamv%d" % l, [128, 4, 64]) for l in range(2)]
    sgain = [din("sgain%d" % l, [128, 128]) for l in range(2)]
    strips = din("strips", [128, 4, STRIP])
    dtile = din("dtile", [128, 5, DSTRIP])
    ident = din("ident", [128, 128])
    xT_o = nc.dram_tensor("xT_o", [D, T], F32, kind="ExternalOutput").ap()
    XSP = nc.dram_tensor("xsp", [D, T], F32)
    HS = [[nc.dram_tensor("hs%d%d" % (l, k), [D, T // 2], BF16) for k in range(2)] for l in range(2)]
    HA = [[nc.dram_tensor("ha%d%d" % (l, k), [2 * D, T // 2], BF16) for k in range(2)] for l in range(2)]
    MS = [[nc.dram_tensor("ms%d%d" % (l, k), [256, S_LEN], BF16) for k in range(2)] for l in range(2)]
    MA = [[nc.dram_tensor("ma%d%d" % (l, k), [2 * 256, S_LEN], BF16) for k in range(2)] for l in range(2)]
    B8D = (nc.dram_tensor("b8a_d", [128, 4, STRIP], BF16), nc.dram_tensor("b8b_d", [128, 4, DSTRIP], BF16))

    def ag(S, src, dst, deps, writes=()):
        return S.add("pool", lambda e: e.collective_compute(
            "AllGather", ALU.bypass, replica_groups=RG_PAIRS, ins=[src.ap().opt()], outs=[dst.ap().opt()]),
            extra_deps=deps, cc=True, writes=list(writes))

    with contextlib.ExitStack() as st:
        sp = SemPool(nc, st)
        PS = st.enter_context(nc.psum_tensor("PS", [128, 8, 512], F32))
        xsp_ap = lambda kc: XSP[ts(kc, 128), :]
        for l in range(2):
            def h_dst(kc, ch, l=l):
                return HS[l][ch][ts(kc, 128), :]

            def h_tail(S, houts, l=l):
                ag(S, HS[l][0], HA[l][0], houts)

            def m_src(kc, half, l=l):
                grp, rk, sub = kc // 4, (kc // 2) % 2, kc % 2
                return MA[l][grp][rk * 256 + sub * 128: rk * 256 + (sub + 1) * 128, half * T:(half + 1) * T]
            if l == 0:
                emit_T(nc, sp, PS, lambda kc: xT[ts(kc, 128), :], xsp_ap, gains, sel, [0, 1],
                       ffnA=ffn[0][0], h_dst=h_dst, tail=h_tail)
            lam0 = 0.8 - 0.6 * math.exp(-0.3 * l)

            def hT_src(hf, kc, ch, l=l):
                return HA[l][ch][hf * D + kc * 128: hf * D + (kc + 1) * 128, :]

            def mix_dst(kind, i, l=l):
                return MS[l][0 if kind == "a" else 1][ts(i, 128), :]

            def tail2(S, outs, l=l):
                ag(S, MS[l][0], MA[l][0], list(outs))

            def tail3(S, outs, l=l):
                ag(S, MS[l][1], MA[l][1], list(outs))
            def head(S, l=l):
                ag(S, HS[l][1], HA[l][1], [], writes=[("HA", 1)])
            emit_B(nc, lam0, None, wB[l], strips, dtile, lamv[l], sgain[l], ident, None, sp=sp, final_wait=True, PS=PS,
                   hT_src=hT_src, tail=tail3, mix_dst=mix_dst, tail2=tail2, head=head,
                   b8_store=(B8D if l == 0 else None), b8_load=(B8D if l == 1 else None))
            if l == 0:
                h_dst1 = lambda kc, ch: HS[1][ch][ts(kc, 128), :]

                def h_tail1(S, houts):
                    ag(S, HS[1][0], HA[1][0], houts)
                emit_T(nc, sp, PS, xsp_ap, xsp_ap, gains, sel, [2, 3, 4], ffnA=ffn[1][0], ffnB=ffn[0][1], wo=wo[0],
                       m_src=m_src, h_dst=h_dst1, tail=h_tail1)
            else:
                emit_T(nc, sp, PS, xsp_ap, lambda kc: xT_o[ts(kc, 128), :], gains, sel, [5, 6], ffnB=ffn[1][1],
                       wo=wo[1], m_src=m_src, final=True)
    return nc


def _ffn_layouts(Wg, Wu, Wd):
    wg_l = np.ascontiguousarray(Wg.reshape(KC, 128, NJ, 128).transpose(2, 1, 0, 3))
    wu_l = np.ascontiguousarray(Wu.reshape(KC, 128, NJ, 128).transpose(2, 1, 0, 3))
    wd_l = np.ascontiguousarray(Wd.reshape(NJ, 128, KC, 128).transpose(2, 1, 0, 3))
    return wg_l, wu_l, wd_l


_NC_CACHE = {}


def kernel(x, ffn1_norm, ffn1_w_gate, ffn1_w_up, ffn1_w_down, mix_norm, w_in,
           lambda_q1, lambda_k1, lambda_q2, lambda_k2, subln_gain, w_out,
           ffn2_norm, ffn2_w_gate, ffn2_w_up, ffn2_w_down, rel_bias, final_norm):
    f32 = np.float32
    A = lambda v: np.asarray(v, f32)
    x = A(x)
    cores = list(range(8))
    if "nc" not in _NC_CACHE:
        _NC_CACHE["nc"] = build_fused()
    nc = _NC_CACHE["nc"]
    gl = np.zeros((128, NG, KC), f32)
    for i, v in enumerate([A(ffn1_norm)[0], A(mix_norm)[0], A(ffn2_norm)[0], A(ffn1_norm)[1], A(mix_norm)[1],
                           A(ffn2_norm)[1], A(final_norm)]):
        gl[:, i, :] = v.reshape(KC, 128).T
    shared = dict(gains=gl, ident=np.eye(128, dtype=f32))
    for l in range(2):
        for k, (wg, wu, wd) in enumerate(((ffn1_w_gate, ffn1_w_up, ffn1_w_down), (ffn2_w_gate, ffn2_w_up, ffn2_w_down))):
            a, b, c_ = _ffn_layouts(A(wg)[l], A(wu)[l], A(wd)[l])
            shared["wg%d%d" % (l, k)], shared["wu%d%d" % (l, k)], shared["wd%d%d" % (l, k)] = a, b, c_
        shared["wo%d" % l] = np.ascontiguousarray(A(w_out)[l].reshape(KC, 128, KC, 128).transpose(2, 1, 0, 3))
        shared["lamv%d" % l] = np.ascontiguousarray(np.broadcast_to(
            np.stack([A(lambda_q1)[l], A(lambda_k1)[l], A(lambda_q2)[l], A(lambda_k2)[l]])[None], (128, 4, 64)))
        shared["sgain%d" % l] = np.ascontiguousarray(np.broadcast_to(A(subln_gain)[l][None], (128, 128)))
    per_g = []
    for g in range(2):
        st_, dt_ = prep_B_consts(A(rel_bias), g)
        d = dict(strips=st_, dtile=dt_)
        for l in range(2):
            d["wB%d" % l] = prep_B_weights(A(w_in)[l], g)
        s = np.zeros((128, 2), f32)
        s[:, g] = 1.0
        d["sel"] = s
        per_g.append(d)
    maps = []
    for c in cores:
        b, r = c // 2, c % 2
        m = dict(shared)
        m.update(per_g[r])
        m["xT"] = np.ascontiguousarray(x[b, r * T:(r + 1) * T, :].T)
        maps.append(m)
    res = run_bass_kernel_spmd(nc, maps, core_ids=cores)
    out = np.empty((4, 4096, D), f32)
    for c in cores:
        out[c // 2, (c % 2) * T:(c % 2 + 1) * T, :] = np.asarray(res.results[c]["xT_o"]).T
    return out
```
